# Optimizing a Trainium2 kernel written in Bass

```python
import math
import jax, jax.numpy as jnp
from jax import lax
import numpy as np

D_MODEL = 1024
BATCH = 4
SEQ = 4096
DEPTH = 2

MEM_LEN = 256
EPS = 1e-6
GDN_HEADS = 4
GDN_DK = 128
GDN_DV = 128
GDN_CONV = 4
GDN_CHUNK = 64
QK_A = GDN_HEADS * GDN_DK
V_A = GDN_HEADS * GDN_DV
QKV_A = 2 * QK_A + V_A
CONV_CH = 512
CONV_K = 31
XA_HEADS = 4
XA_DH = 128
XA_W = XA_HEADS * XA_DH
N_BRANCH = 3
FFN_DIM = 2816
FFN_CONV = 3
IN_DIM = QKV_A + GDN_HEADS + GDN_HEADS + V_A + 2 * CONV_CH + XA_W + N_BRANCH * D_MODEL

kernel_name = "hybrid_gdn_conformer_xattn_block"


def _split_points():
    p0 = QKV_A
    p1 = p0 + GDN_HEADS
    p2 = p1 + GDN_HEADS
    p3 = p2 + V_A
    p4 = p3 + 2 * CONV_CH
    p5 = p4 + XA_W
    return [p0, p1, p2, p3, p4, p5]


def rmsnorm(x, w):
    xf = x.astype(jnp.float32)
    y = xf * lax.rsqrt(jnp.mean(xf * xf, axis=-1, keepdims=True) + EPS)
    return (y * w.astype(jnp.float32)).astype(x.dtype)


def layernorm(x, w, b):
    xf = x.astype(jnp.float32)
    mu = jnp.mean(xf, axis=-1, keepdims=True)
    var = jnp.mean(jnp.square(xf - mu), axis=-1, keepdims=True)
    y = (xf - mu) * lax.rsqrt(var + EPS)
    return (y * w.astype(jnp.float32) + b.astype(jnp.float32)).astype(x.dtype)


def causal_dwconv(x, w):
    K, C = w.shape
    return lax.conv_general_dilated(
        x, w[:, None, :].astype(x.dtype), window_strides=(1,), padding=[(K - 1, 0)],
        dimension_numbers=('NWC', 'WIO', 'NWC'), feature_group_count=C)


def l2norm(x):
    xf = x.astype(jnp.float32)
    return xf * lax.rsqrt(jnp.sum(xf * xf, axis=-1, keepdims=True) + EPS)


def gated_delta_rule(q, k, v, g, beta):
    Bn, Sn, H, dk = q.shape
    dv = v.shape[-1]
    C = GDN_CHUNK
    N = Sn // C
    f32 = jnp.float32

    def chunk(t):
        t = t.astype(f32).reshape((Bn, N, C, H) + t.shape[3:])
        return jnp.swapaxes(t, 2, 3)

    q = chunk(q) * (dk ** -0.5)
    k = chunk(k)
    v = chunk(v)
    beta = chunk(beta)
    g = jnp.cumsum(chunk(g), axis=-1)
    idx = jnp.arange(C)
    causal = idx[:, None] >= idx[None, :]
    strict = idx[:, None] > idx[None, :]
    decay = jnp.exp(jnp.where(causal, g[..., :, None] - g[..., None, :], -jnp.inf))
    kb = k * beta[..., None]
    A = jnp.where(strict, jnp.einsum('bnhid,bnhjd->bnhij', kb, k) * decay, 0.0)
    eye = jnp.eye(C, dtype=f32)
    T = lax.linalg.triangular_solve(eye + A, jnp.broadcast_to(eye, A.shape), left_side=True, lower=True)
    u = jnp.einsum('bnhij,bnhje->bnhie', T, v * beta[..., None])
    w = jnp.einsum('bnhij,bnhjd->bnhid', T, kb * jnp.exp(g)[..., None])
    qk = jnp.where(causal, jnp.einsum('bnhid,bnhjd->bnhij', q, k) * decay, 0.0)
    q_dec = q * jnp.exp(g)[..., None]
    g_last = g[..., -1]
    k_dec = k * jnp.exp(g_last[..., None] - g)[..., None]

    def step(state, inp):
        qd_c, kd_c, u_c, w_c, qk_c, gl_c = inp
        v_new = u_c - jnp.einsum('bhcd,bhde->bhce', w_c, state)
        o = jnp.einsum('bhcd,bhde->bhce', qd_c, state) + jnp.einsum('bhij,bhje->bhie', qk_c, v_new)
        state = state * jnp.exp(gl_c)[..., None, None] + jnp.einsum('bhcd,bhce->bhde', kd_c, v_new)
        return state, o

    xs = tuple(jnp.moveaxis(t, 1, 0) for t in (q_dec, k_dec, u, w, qk, g_last))
    s0 = jnp.zeros((Bn, H, dk, dv), f32)
    _, o = lax.scan(step, s0, xs)
    return jnp.transpose(o, (1, 0, 3, 2, 4)).reshape(Bn, Sn, H, dv)


def setup_inputs(seed: int = 0) -> dict:
    key = jax.random.key(seed)
    ks = jax.random.split(key, 32)
    f32 = jnp.float32
    L = DEPTH

    def nrm(k, shape, fan_in):
        return jax.random.normal(k, shape, f32) * (fan_in ** -0.5)

    def gain(k, shape):
        return 1.0 + 0.02 * jax.random.normal(k, shape, f32)

    def bias(k, shape):
        return 0.02 * jax.random.normal(k, shape, f32)

    dt = jnp.exp(jax.random.uniform(ks[5], (L, GDN_HEADS), f32, math.log(1e-3), math.log(1e-1)))
    return {
        'x': jax.random.normal(ks[0], (BATCH, SEQ, D_MODEL), f32),
        'mem': jax.random.normal(ks[1], (BATCH, MEM_LEN, D_MODEL), f32),
        'norm_mix': gain(ks[2], (L, D_MODEL)),
        'w_in': nrm(ks[3], (L, D_MODEL, IN_DIM), D_MODEL),
        'gdn_conv_w': nrm(ks[4], (L, GDN_CONV, QKV_A), GDN_CONV),
        'gdn_dt_bias': dt + jnp.log(-jnp.expm1(-dt)),
        'gdn_a_log': jnp.log(jax.random.uniform(ks[6], (L, GDN_HEADS), f32, 1.0, 16.0)),
        'gdn_norm': gain(ks[7], (L, GDN_DV)),
        'w_gdn_out': nrm(ks[8], (L, V_A, D_MODEL), V_A),
        'cc_glu_b': bias(ks[9], (L, 2 * CONV_CH)),
        'cc_dw_w': nrm(ks[10], (L, CONV_K, CONV_CH), CONV_K),
        'cc_dw_b': bias(ks[11], (L, CONV_CH)),
        'cc_ln_w': gain(ks[12], (L, CONV_CH)),
        'cc_ln_b': bias(ks[13], (L, CONV_CH)),
        'w_cc_out': nrm(ks[14], (L, CONV_CH, D_MODEL), CONV_CH),
        'mem_norm': gain(ks[15], (L, D_MODEL)),
        'w_mem_kv': nrm(ks[16], (L, D_MODEL, 2 * XA_W), D_MODEL),
        'w_xa_out': nrm(ks[17], (L, XA_W, D_MODEL), XA_W),
        'gate_b': bias(ks[18], (L, N_BRANCH * D_MODEL)),
        'w_o': nrm(ks[19], (L, D_MODEL, D_MODEL), D_MODEL),
        'norm_ffn': gain(ks[20], (L, D_MODEL)),
        'w_up': nrm(ks[21], (L, D_MODEL, 2 * FFN_DIM), D_MODEL),
        'ffn_dw_w': nrm(ks[22], (L, FFN_CONV, FFN_DIM), FFN_CONV),
        'ffn_dw_b': bias(ks[23], (L, FFN_DIM)),
        'w_down': nrm(ks[24], (L, FFN_DIM, D_MODEL), FFN_DIM),
        'norm_final': gain(ks[25], (D_MODEL,)),
    }


def reference(x, mem, norm_mix, w_in, gdn_conv_w, gdn_dt_bias, gdn_a_log, gdn_norm, w_gdn_out,
              cc_glu_b, cc_dw_w, cc_dw_b, cc_ln_w, cc_ln_b, w_cc_out,
              mem_norm, w_mem_kv, w_xa_out, gate_b, w_o,
              norm_ffn, w_up, ffn_dw_w, ffn_dw_b, w_down, norm_final):
    Bn, Sn, D = x.shape
    Mn = mem.shape[1]
    dt = x.dtype
    f32 = jnp.float32
    for l in range(DEPTH):
        h = rmsnorm(x, norm_mix[l])
        proj = h @ w_in[l]
        qkv_a, a_a, b_a, z_a, glu_in, q_c, gate_logits = jnp.split(proj, _split_points(), axis=-1)

        qkv_a = jax.nn.silu(causal_dwconv(qkv_a, gdn_conv_w[l]))
        q_a, k_a, v_a = jnp.split(qkv_a, [QK_A, 2 * QK_A], axis=-1)
        q_a = l2norm(q_a.reshape(Bn, Sn, GDN_HEADS, GDN_DK))
        k_a = l2norm(k_a.reshape(Bn, Sn, GDN_HEADS, GDN_DK))
        v_a = v_a.reshape(Bn, Sn, GDN_HEADS, GDN_DV)
        g_a = -jnp.exp(gdn_a_log[l].astype(f32)) * jax.nn.softplus(a_a.astype(f32) + gdn_dt_bias[l].astype(f32))
        beta_a = jax.nn.sigmoid(b_a.astype(f32))
        o_a = gated_delta_rule(q_a, k_a, v_a, g_a, beta_a).astype(dt)
        o_a = rmsnorm(o_a, gdn_norm[l]) * jax.nn.silu(z_a.reshape(Bn, Sn, GDN_HEADS, GDN_DV))
        y_a = o_a.reshape(Bn, Sn, V_A) @ w_gdn_out[l]

        glu = glu_in + cc_glu_b[l]
        u = glu[..., :CONV_CH] * jax.nn.sigmoid(glu[..., CONV_CH:])
        u = causal_dwconv(u, cc_dw_w[l]) + cc_dw_b[l]
        u = jax.nn.silu(layernorm(u, cc_ln_w[l], cc_ln_b[l]))
        y_b = u @ w_cc_out[l]

        kv_m = rmsnorm(mem, mem_norm[l]) @ w_mem_kv[l]
        k_m = kv_m[..., :XA_W].reshape(Bn, Mn, XA_HEADS, XA_DH)
        v_m = kv_m[..., XA_W:].reshape(Bn, Mn, XA_HEADS, XA_DH)
        q_m = q_c.reshape(Bn, Sn, XA_HEADS, XA_DH)
        s = jnp.einsum('bshd,bmhd->bhsm', q_m, k_m).astype(f32) * (XA_DH ** -0.5)
        p = jax.nn.softmax(s, axis=-1).astype(dt)
        o_c = jnp.einsum('bhsm,bmhd->bshd', p, v_m).reshape(Bn, Sn, XA_W)
        y_c = o_c @ w_xa_out[l]

        gates = jax.nn.sigmoid((gate_logits + gate_b[l]).astype(f32)).astype(dt).reshape(Bn, Sn, N_BRANCH, D)
        merged = gates[..., 0, :] * y_a + gates[..., 1, :] * y_b + gates[..., 2, :] * y_c
        x = x + merged @ w_o[l]

        h = rmsnorm(x, norm_ffn[l])
        up = h @ w_up[l]
        g_f = causal_dwconv(up[..., :FFN_DIM], ffn_dw_w[l]) + ffn_dw_b[l]
        x = x + (jax.nn.silu(g_f) * up[..., FFN_DIM:]) @ w_down[l]
    return rmsnorm(x, norm_final)
```

```python
import numpy as np
from contextlib import ExitStack
import concourse.bass as bass
import concourse.mybir as mybir
from concourse.bass_utils import run_bass_kernel_spmd

F32 = mybir.dt.float32
BF16 = mybir.dt.bfloat16
AF = mybir.ActivationFunctionType
ALU = mybir.AluOpType

D = 1024
KC = 8
TB = 512
EPS = 1e-6
DK = 128
FFN = 2816
NF = 22
NGA = 68
WSLOT = 2816
NEGBIG = -30000.0

PP_NMIX = 0
PP_GCW = 8
PP_GLUB = 56
PP_CCW = 64
PP_CCB = 188
PP_LNW = 192
PP_LNB = 196
PP_MEMN = 200
PP_GATEB = 208
PP_NFFN = 232
PP_FFW = 240
PP_FFB = 306
PP_GN = 328
PP_NFIN = 329
NPP = 337

C_ID, C_ONE, C_MLE, C_NLS, C_NUI, C_SELL, C_SC0, C_SC1 = range(8)
NCST = 8

import os
INTERLEAVE = os.environ.get('K_INTER', '1') == '1'
PIPE = os.environ.get('K_PIPE', '0') == '1'
A_PER_R = int(os.environ.get('K_APR', '6'))
YMASK = int(os.environ.get('K_YMASK', '1984'))
SEM_EPOCH = 16000
N_DMA_SEMS = 24


class Buf:
    __slots__ = ("lw", "rd")

    def __init__(self):
        self.lw = None
        self.rd = {}


class Tl:
    __slots__ = ("ap", "b")

    def __init__(self, ap, b=None):
        self.ap = ap
        self.b = b if b is not None else Buf()

    def __getitem__(self, k):
        return Tl(self.ap[k], self.b)


class KB:
    def __init__(self, nc, stack):
        self.nc = nc
        self.stack = stack
        self.eng = {"pe": nc.tensor, "act": nc.scalar, "dve": nc.vector, "pool": nc.gpsimd, "sp": nc.sync}
        self.sems = {}
        self.cur = {}
        self.epoch = {}
        for e in self.eng:
            self.epoch[e] = 0
            self._new_epoch(e)
        self.seen = {e: {} for e in self.eng}
        self.dsem = []
        for i in range(N_DMA_SEMS):
            k = f"d{i}"
            self.sems[k] = stack.enter_context(nc.semaphore(k))
            self.dsem.append([k, 0])
        self.dnext = 0
        self.qsem = []
        for i in range(8):
            k = f"q{i}"
            self.sems[k] = stack.enter_context(nc.semaphore(k))
            self.qsem.append([k, 0])
        self.qnext = 0
        self.n_instr = 0

    def _new_epoch(self, e):
        k = f"{e}_{self.epoch[e]}"
        self.sems[k] = self.stack.enter_context(self.nc.semaphore(k))
        self.cur[e] = [k, 0]
        self.epoch[e] += 1

    def _wait(self, e, ev):
        k, v = ev
        if self.seen[e].get(k, 0) >= v:
            return
        if e == "pe" and k.startswith("pe_"):
            return
        self.eng[e].wait_ge(self.sems[k], v)
        self.seen[e][k] = v

    @staticmethod
    def _flat(bs):
        o = []
        for b in bs:
            if isinstance(b, (list, tuple)):
                o.extend(b)
            else:
                o.append(b)
        return o

    def _deps(self, e, reads, writes):
        reads = self._flat(reads)
        writes = self._flat(writes)
        for b in reads:
            if b.lw is not None:
                self._wait(e, b.lw)
        for b in writes:
            if b.lw is not None:
                self._wait(e, b.lw)
            for k, v in b.rd.items():
                self._wait(e, (k, v))

    def _mark(self, ev, reads, writes):
        reads = self._flat(reads)
        writes = self._flat(writes)
        for b in writes:
            b.lw = ev
            b.rd = {}
        for b in reads:
            b.rd[ev[0]] = ev[1]

    def op(self, e, fn, reads=(), writes=()):
        self._deps(e, reads, writes)
        ins = fn(self.eng[e])
        c = self.cur[e]
        c[1] += 1
        ins.then_inc(self.sems[c[0]], 1)
        self._mark((c[0], c[1]), reads, writes)
        self.n_instr += 1
        if c[1] >= SEM_EPOCH:
            self._new_epoch(e)

    def dma(self, q, out, in_, reads=(), writes=()):
        if q == "pool":
            d = self.qsem[self.qnext]
            self.qnext = (self.qnext + 1) % len(self.qsem)
        else:
            d = self.dsem[self.dnext]
            self.dnext = (self.dnext + 1) % len(self.dsem)
        if d[1] > 0:
            self._wait(q, (d[0], d[1]))
        self._deps(q, reads, writes)
        ins = self.eng[q].dma_start(out=out, in_=in_)
        d[1] += 16
        ins.then_inc(self.sems[d[0]], 16)
        self._mark((d[0], d[1]), reads, writes)
        self.n_instr += 1

    def finish(self, bufs):
        for b in bufs:
            if b.lw is not None:
                self._wait("sp", b.lw)


def build(T, NL, dbg_names=()):
    NBLK = T // TB
    nc = bass.Bass("TRN2", target_bir_lowering=False)
    x_in = nc.dram_tensor("x", [NBLK, 128, KC * TB], F32, kind="ExternalInput").ap()
    mem_in = nc.dram_tensor("mem", [2, 128, D], F32, kind="ExternalInput").ap()
    cst_in = nc.dram_tensor("cst", [128, NCST * 128], F32, kind="ExternalInput").ap()
    wa_in = [nc.dram_tensor(f"wa{l}", [NGA, 128, 2048], F32, kind="ExternalInput").ap() for l in range(NL)]
    wd_in = [nc.dram_tensor(f"wd{l}", [8, 128, WSLOT], F32, kind="ExternalInput").ap() for l in range(NL)]
    pp_in = [nc.dram_tensor(f"pp{l}", [128, NPP], F32, kind="ExternalInput").ap() for l in range(NL)]
    pb_in = [nc.dram_tensor(f"pb{l}", [128, 8], F32, kind="ExternalInput").ap() for l in range(NL)]
    wab_in = [nc.dram_tensor(f"wab{l}", [128, 64], F32, kind="ExternalInput").ap() for l in range(NL)]
    out_d = nc.dram_tensor("out", [NBLK, 128, KC * TB], F32, kind="ExternalOutput").ap()
    wabf = [nc.dram_tensor(f"wabf{l}", [NGA, 128, 2048], BF16).ap() for l in range(NL)]
    wdbf = [nc.dram_tensor(f"wdbf{l}", [8, 128, WSLOT], BF16).ap() for l in range(NL)]
    xs = [x_in] + [nc.dram_tensor(f"xs{l}", [NBLK, 128, KC * TB], F32).ap() for l in range(1, NL)] + [out_d]
    xs_b = [[Buf() for _ in range(NBLK)] for _ in range(NL + 1)]
    dbg_out = {}

    with ExitStack() as st:
        kb = KB(nc, st)

        def sbt(name, shape, dt):
            return st.enter_context(nc.sbuf_tensor("sb_" + name, shape, dt))

        def rb(*tl):
            return [t.b for t in tl if isinstance(t, Tl)]

        def apof(v):
            return v.ap if isinstance(v, Tl) else v

        def mm(ps, lhsT, rhs, start=True, stop=True):
            kb.op("pe", lambda e: e.matmul(ps.ap, lhsT.ap, rhs.ap, start=start, stop=stop),
                  reads=[lhsT.b, rhs.b], writes=[ps.b])

        def tr(ps, in_, idn):
            kb.op("pe", lambda e: e.transpose(ps.ap, in_.ap, idn.ap), reads=[in_.b, idn.b], writes=[ps.b])

        def act(out, in_, func, bias=None, scale=None, accum=None, eng="act"):
            kw = {}
            if bias is not None:
                kw["bias"] = apof(bias)
            if scale is not None:
                kw["scale"] = apof(scale)
            wr = [out.b]
            if accum is not None:
                kw["accum_out"] = accum.ap
                wr.append(accum.b)
            kb.op("act", lambda e: e.activation(out=out.ap, in_=in_.ap, func=func, **kw),
                  reads=[in_.b] + rb(bias, scale), writes=wr)

        def tt(out, a, b, op, eng="dve"):
            kb.op(eng, lambda e: e.tensor_tensor(out=out.ap, in0=a.ap, in1=b.ap, op=op), reads=[a.b, b.b], writes=[out.b])

        def ts(out, a, s1, op0, s2=None, op1=None, eng="dve"):
            if op1 is None:
                kb.op(eng, lambda e: e.tensor_scalar(out=out.ap, in0=a.ap, scalar1=apof(s1), scalar2=None, op0=op0),
                      reads=[a.b] + rb(s1), writes=[out.b])
            else:
                kb.op(eng, lambda e: e.tensor_scalar(out=out.ap, in0=a.ap, scalar1=apof(s1), scalar2=apof(s2), op0=op0, op1=op1),
                      reads=[a.b] + rb(s1, s2), writes=[out.b])

        def stt(out, in0, scalar, in1, op0, op1):
            kb.op("dve", lambda e: e.scalar_tensor_tensor(out=out.ap, in0=in0.ap, scalar=apof(scalar), in1=in1.ap, op0=op0, op1=op1),
                  reads=[in0.b, in1.b] + rb(scalar), writes=[out.b])

        def rsqrt(out, in_, scale, bias):
            act(out, in_, AF.Ln, bias=bias, scale=scale)
            act(out, out, AF.Exp, scale=-0.5)

        def recip(out, in_):
            kb.op("dve", lambda e: e.reciprocal(out=out.ap, in_=in_.ap), reads=[in_.b], writes=[out.b])

        def cp(out, in_, eng="act"):
            if eng == "act":
                kb.op("act", lambda e: e.copy(out=out.ap, in_=in_.ap), reads=[in_.b], writes=[out.b])
            else:
                kb.op(eng, lambda e: e.tensor_copy(out=out.ap, in_=in_.ap), reads=[in_.b], writes=[out.b])

        def memset(t, v, eng="dve"):
            kb.op(eng, lambda e: e.memset(t.ap, v), writes=[t.b])

        def dbg(name, t, shape, extra=()):
            if name not in dbg_names:
                return
            t = Tl(t.ap, t.b)
            for e_ in extra:
                kb._deps("pool", [e_.b], [])
            d = nc.dram_tensor("dbg_" + name, list(shape), F32, kind="ExternalOutput").ap()
            tmp = Buf()
            kb.dma("pool", d, t.ap, reads=[t.b], writes=[tmp])
            dbg_out[name] = tmp

        cst_t = sbt("cst", [128, NCST * 128], F32)
        cstb = Buf()
        kb.dma("sp", cst_t[:], cst_in, writes=[cstb])
        CST = [Tl(cst_t[:, i * 128:(i + 1) * 128], cstb) for i in range(NCST)]
        ident32, ones32 = CST[C_ID], CST[C_ONE]
        cbf_t = sbt("cbf", [128, 256], BF16)
        ident_bf = Tl(cbf_t[:, 0:128])
        ones_bf = Tl(cbf_t[:, 128:256])
        cp(ident_bf, ident32)
        cp(ones_bf, ones32)

        psum_tiles = [Tl(st.enter_context(nc.psum_tensor(f"ps{i}", [128, 512], F32))[:]) for i in range(8)]
        psn = [0]

        def ps():
            t = psum_tiles[psn[0] % 8]
            psn[0] += 1
            return t

        def psb():
            t = ps()
            return Tl(t.ap.bitcast(BF16), t.b)

        xT_t = sbt("xT", [128, KC, TB], F32)
        xT = [Tl(xT_t[:, c, :]) for c in range(KC)]
        xT_all = Tl(xT_t[:].rearrange("p c t -> p (c t)"))
        hT_t = sbt("hT", [128, KC, TB], BF16)
        hT = [Tl(hT_t[:, c, :]) for c in range(KC)]
        qkv_t = sbt("qkvT", [128, 8, TB], F32)
        qkvT = [Tl(qkv_t[:, j, :]) for j in range(8)]
        qkb_t = sbt("qkb", [128, 12, TB], BF16)
        qkb = [Tl(qkb_t[:, j, :]) for j in range(12)]
        zT_t = sbt("zT", [128, 4, TB], BF16)
        zT = [Tl(zT_t[:, j, :]) for j in range(4)]
        oa_t = sbt("oaT", [128, 4, TB], BF16)
        oaT = [Tl(oa_t[:, j, :]) for j in range(4)]
        uc_t = sbt("ucT", [128, 4, TB], BF16)
        ucT = [Tl(uc_t[:, j, :]) for j in range(4)]
        oc_t = sbt("ocT", [128, 4, TB], BF16)
        ocT = [Tl(oc_t[:, j, :]) for j in range(4)]
        qcT = zT
        af_t = sbt("actf", [128, NF, TB], BF16)
        actf = [Tl(af_t[:, j, :]) for j in range(NF)]
        mgT = actf[0:KC]
        mg_t = af_t
        big_t = sbt("big", [128, 6, TB], F32)
        bigs = [Tl(big_t[:, j, :]) for j in range(6)]
        bign = [0]

        def big():
            t = bigs[bign[0] % 6]
            bign[0] += 1
            return t

        def bigb():
            t = big()
            return Tl(t.ap.bitcast(BF16)[:, 0:TB], t.b)

        pre_t = sbt("pre", [128, 2, 544], BF16)
        pres = [Tl(pre_t[:, j, :]) for j in range(2)]
        pren = [0]
        ub_t = sbt("ubuf", [128, 4, 544], BF16)
        ubuf = [Tl(ub_t[:, j, :]) for j in range(4)]
        hg_t = sbt("halo_g", [128, 12, 4], BF16)
        halo_g = [Tl(hg_t[:, j, :]) for j in range(12)]
        hf_t = sbt("halo_f", [128, NF, 2], BF16)
        halo_f = [Tl(hf_t[:, j, :]) for j in range(NF)]
        dg_t = sbt("diag", [128, 16, 128], BF16)
        dgs = [Tl(dg_t[:, j, :]) for j in range(16)]
        dgn = [0]
        w_t = sbt("wring", [128, 3, WSLOT], BF16)
        wslots = [Tl(w_t[:, j, :]) for j in range(3)]
        S_t = sbt("S", [128, 4, 128], F32)
        S = [Tl(S_t[:, h, :]) for h in range(4)]
        E_t = sbt("E", [128, 2, TB], BF16)
        Ebuf = [Tl(E_t[:, j, :]) for j in range(2)]
        km_t = sbt("kmT", [128, 4, 256], BF16)
        kmT = [Tl(km_t[:, h, :]) for h in range(4)]
        vm_t = sbt("vm", [128, 2, 512], BF16)
        vm = [Tl(vm_t[:, j, :]) for j in range(2)]
        mn_t = sbt("memnT", [128, KC, 256], BF16)
        memnT = [Tl(mn_t[:, c, :]) for c in range(KC)]
        pp_t = [sbt(f"ppT{l}", [128, NPP], F32) for l in range(NL)]
        ppT = [Tl(pp_t[l][:]) for l in range(NL)]
        pb_t = [sbt(f"pbT{l}", [128, 8], F32) for l in range(NL)]
        pbT = [Tl(pb_t[l][:]) for l in range(NL)]
        wab_t = [sbt(f"wabT{l}", [128, KC, 8], BF16) for l in range(NL)]
        wabT = [Tl(wab_t[l][:]) for l in range(NL)]
        negA_t = sbt("negA", [128, 4], F32)
        negA = Tl(negA_t[:])
        gs_names = ["y", "ay", "e1", "l1", "m", "g", "beta", "nbeta", "gcum", "ngcum", "sckd", "egc", "scw", "egl0", "egl1", "tmp"]
        gs_t = sbt("gs", [128, len(gs_names), 16], F32)
        gs = {n: Tl(gs_t[:, i, :]) for i, n in enumerate(gs_names)}
        names_h = ["u", "wT", "qd", "QKT", "kdec", "vnew", "o", "on"]
        gh_t = sbt("gh", [128, len(names_h) * 4, 128], F32)
        GH = {n: [Tl(gh_t[:, i * 4 + h, :]) for h in range(4)] for i, n in enumerate(names_h)}
        gy = [Tl(gh_t[:, 4 * j:4 * j + 4, :].rearrange("p a t -> p (a t)"),
                 [GH[names_h[j]][h].b for h in range(4)]) for j in range(8)]
        names_t = ["diagG", "t1", "Dl", "t2", "Du", "eg"]
        names_tb = ["kbg", "vb", "Pa", "Pb"]
        gtb_t = sbt("gtb", [128, 4 * len(names_tb), 128], BF16)
        names_hb = ["wT", "qd", "QKT", "kdec", "vnew", "on"]
        ghb_t = sbt("ghb", [128, len(names_hb) * 4, 128], BF16)
        sqb_t = sbt("sqb", [128, 8, TB], BF16)
        sqb = [Tl(sqb_t[:, j, :]) for j in range(8)]
        sqd_t = sbt("sqdummy", [128, 128], BF16)
        sqdummy = Tl(sqd_t[:])
        Sb_t = sbt("Sb", [128, 4, 128], BF16)
        Sb = [Tl(Sb_t[:, h, :]) for h in range(4)]
        gt_t = sbt("gt", [128, 4 * len(names_t), 128], F32)
        GT = [{n: Tl(gt_t[:, i * 4 + s, :]) for i, n in enumerate(names_t)} for s in range(4)]
        for s_ in range(4):
            for i, n in enumerate(names_tb):
                GT[s_][n] = Tl(gtb_t[:, i * 4 + s_, :])

        def wide(tensor, idx0, tls):
            return Tl(tensor[:, idx0:idx0 + 4, :], [t_.b for t_ in tls])
        W4 = {n: wide(gt_t, i * 4, [GT[h_][n] for h_ in range(4)]) for i, n in enumerate(names_t)}
        for i, n in enumerate(names_tb):
            W4[n] = wide(gtb_t, i * 4, [GT[h_][n] for h_ in range(4)])
        _ghb0 = {n: [Tl(ghb_t[:, i * 4 + h, :]) for h in range(4)] for i, n in enumerate(names_hb)}
        GHbS = [_ghb0, _ghb0]
        GuS = [GH["u"], GH["u"]]
        qx_t = sbt("qx", [128, 2, 4, 256], BF16)
        QX = [[Tl(qx_t[:, i, s, :]) for i in range(2)] for s in range(4)]
        QX4 = [Tl(qx_t[:, i, :, :], [QX[h_][i].b for h_ in range(4)]) for i in range(2)]
        on4_t = sbt("ones4", [128, 4, 128], F32)
        ones4 = [Tl(on4_t[:, h, :]) for h in range(4)]
        for h in range(4):
            memset(ones4[h], 1.0)
        ss_t = sbt("ss", [128, 16], F32)
        ss = Tl(ss_t[:, 0:4]); ssd = Tl(ss_t[:, 4:8]); srs = Tl(ss_t[:, 8:12])
        mss = Tl(ss_t[:, 12:13]); msd = Tl(ss_t[:, 13:14]); mrs = Tl(ss_t[:, 14:15])

        wa_b = [[Buf() for _ in range(NGA)] for _ in range(NL)]
        wd_b = [[Buf() for _ in range(8)] for _ in range(NL)]
        blk_seq = [("a", g) for g in range(14)]
        for m in range(4):
            blk_seq += [("a", 26 + m), ("a", 30 + m), ("a", 34 + m), ("a", 14 + m), ("a", 18 + m), ("a", 22 + m)]
        blk_seq += [("a", 38 + m) for m in range(4)]
        for fg in range(11):
            blk_seq += [("a", 42 + fg), ("a", 53 + fg)]
        blk_seq += [("d", o) for o in range(8)]
        lay_seq = [("a", 64), ("a", 65), ("a", 66), ("a", 67)]
        allseq = []
        for l in range(NL):
            allseq += [(l,) + s for s in lay_seq]
            for b in range(NBLK):
                allseq += [(l,) + s for s in blk_seq]
        done = set()
        for (l, kind, g) in allseq:
            if (l, kind, g) in done:
                continue
            done.add((l, kind, g))
            if kind == "a":
                kb.dma("pool", wabf[l][g], wa_in[l][g], writes=[wa_b[l][g]])
            else:
                kb.dma("pool", wdbf[l][g], wd_in[l][g], writes=[wd_b[l][g]])

        wstate = {"issued": 0, "next": 0}
        LOOK = 2

        def nextw(l, kind, g):
            i = wstate["next"]
            assert allseq[i] == (l, kind, g), (allseq[i], (l, kind, g))
            while wstate["issued"] < min(i + 1 + LOOK, len(allseq)):
                j = wstate["issued"]
                (l2, k2, g2) = allseq[j]
                slot = wslots[j % 3]
                if k2 == "a":
                    kb.dma("sp", slot.ap[:, 0:2048], wabf[l2][g2], reads=[wa_b[l2][g2]], writes=[slot.b])
                else:
                    kb.dma("sp", slot.ap, wdbf[l2][g2], reads=[wd_b[l2][g2]], writes=[slot.b])
                wstate["issued"] += 1
            wstate["next"] += 1
            slot = wslots[i % 3]
            if kind == "a":
                return Tl(slot.ap[:, 0:2048].rearrange("p (k n) -> p k n", k=KC), slot.b)
            return Tl(slot.ap.rearrange("p (f n) -> p f n", f=NF), slot.b)

        def ppc(l, col, n=1):
            return Tl(pp_t[l][:, col:col + n], ppT[l].b)

        def diag(l, col):
            d = dgs[dgn[0] % 16]
            dgn[0] += 1
            if dgn[0] % 2 == 0:
                kb.op("dve", lambda e: e.tensor_scalar(out=d.ap, in0=ident_bf.ap, scalar1=pp_t[l][:, col:col + 1], scalar2=None, op0=ALU.mult),
                      reads=[ident_bf.b, ppT[l].b], writes=[d.b])
            else:
                kb.op("act", lambda e: e.activation(out=d.ap, in_=ident_bf.ap, func=AF.Identity, scale=pp_t[l][:, col:col + 1]),
                      reads=[ident_bf.b, ppT[l].b], writes=[d.b])
            return d

        def proj(W, cols, src, nk):
            p = ps()
            for kc in range(nk):
                mm(p, W[:, kc, cols], src[kc], start=(kc == 0), stop=(kc == nk - 1))
            return p

        def rmsnorm(l, wcol, outs):
            pss = ps()
            for c in range(KC):
                sq = bigb()
                act(sq, xT[c], AF.Square)
                mm(pss, ones_bf, sq, start=(c == 0), stop=(c == KC - 1))
            rstd = big()
            rsqrt(rstd, pss, 1.0 / D, EPS)
            for c in range(KC):
                stt(outs[c], xT[c], ppc(l, wcol + c), rstd, ALU.mult, ALU.mult)

        for l in range(NL):
            kb.dma("sp", pp_t[l][:], pp_in[l], writes=[ppT[l].b])
            kb.dma("sp", pb_t[l][:], pb_in[l], writes=[pbT[l].b])
            kb.dma("pool", wab_t[l][:].rearrange("p k n -> p (k n)"), wab_in[l], writes=[wabT[l].b])
            act(negA, Tl(pb_t[l][:, 4:8], pbT[l].b), AF.Exp)
            ts(negA, negA, -1.0, ALU.mult)
            for h in range(4):
                memset(S[h], 0.0)
                memset(Sb[h], 0.0)
            for j in range(12):
                memset(halo_g[j], 0.0, eng="pool")
            for f in range(NF):
                memset(halo_f[f], 0.0, eng="pool")
            for c in range(4):
                memset(ubuf[c][:, 0:32], 0.0, eng="pool")
            memt = [Tl(qkv_t[:, 2 * mt:2 * mt + 2, :].rearrange("p a t -> p (a t)"), qkvT[2 * mt].b) for mt in range(2)]
            for mt in range(2):
                kb.dma("sp", memt[mt].ap, mem_in[mt], writes=[memt[mt].b, qkvT[2 * mt + 1].b])
            for mt in range(2):
                sq = big()
                for hh in range(2):
                    act(sq, memt[mt][:, hh * 512:(hh + 1) * 512], AF.Square, accum=Tl(ss_t[:, 12 + hh:13 + hh], mss.b))
                tt(mss, Tl(ss_t[:, 12:13], mss.b), Tl(ss_t[:, 13:14], mss.b), ALU.add)
                act(msd, mss, AF.Sqrt, bias=EPS, scale=1.0 / D)
                recip(mrs, msd)
                for hh in range(2):
                    act(memt[mt][:, hh * 512:(hh + 1) * 512], memt[mt][:, hh * 512:(hh + 1) * 512], AF.Identity, scale=mrs)
            for c in range(KC):
                p = ps()
                for mt in range(2):
                    tr(p[:, mt * 128:(mt + 1) * 128], memt[mt][:, c * 128:(c + 1) * 128], ident32)
                ts(memnT[c], p[:, 0:256], ppc(l, PP_MEMN + c), ALU.mult)
            for g in range(2):
                W = nextw(l, "a", 64 + g)
                for hh in range(2):
                    h = 2 * g + hh
                    p = ps()
                    for kc in range(KC):
                        mm(p[:, 0:256], W[:, kc, hh * 128:(hh + 1) * 128], memnT[kc], start=(kc == 0), stop=(kc == KC - 1))
                    cp(kmT[h], p[:, 0:256])
            for g in range(2):
                W = nextw(l, "a", 66 + g)
                for mt in range(2):
                    p = ps()
                    for kc in range(KC):
                        mm(p[:, 0:256], memnT[kc][:, mt * 128:(mt + 1) * 128], W[:, kc, :], start=(kc == 0), stop=(kc == KC - 1))
                    cp(vm[mt][:, g * 256:(g + 1) * 256], p[:, 0:256])

            for b in range(NBLK):
                kb.dma("sp", xT_t[:].rearrange("p c t -> p (c t)"), xs[l][b], reads=[xs_b[l][b]], writes=[t.b for t in xT])
                rmsnorm(l, PP_NMIX, hT)

                pab = ps()
                for t in range(4):
                    for kc in range(KC):
                        mm(pab[:, t * 8:(t + 1) * 8], hT[kc][:, t * 128:(t + 1) * 128], wabT[l][:, kc, :], start=(kc == 0), stop=(kc == KC - 1))
                pab3 = Tl(pab.ap[:, 0:32].rearrange("p (t e) -> p t e", t=4), pab.b)
                v3 = lambda t_: Tl(t_.ap.rearrange("p (t e) -> p t e", t=4), t_.b)
                dtb = Tl(pb_t[l][:, 0:4].unsqueeze(1).to_broadcast([128, 4, 4]), pbT[l].b)
                nA3 = Tl(negA_t[:, 0:4].unsqueeze(1).to_broadcast([128, 4, 4]), negA.b)
                tt(v3(gs["y"]), pab3[:, :, 0:4], dtb, ALU.add)
                act(v3(gs["beta"]), pab3[:, :, 4:8], AF.Sigmoid)
                pch = {}

                def c_gcum():
                    pch["g"] = ps()
                    mm(pch["g"][:, 0:16], CST[C_MLE], gs["g"])
                    cp(gs["gcum"], pch["g"][:, 0:16])

                def c_sell():
                    pch["l"] = ps()
                    mm(pch["l"][:, 0:16], CST[C_SELL], gs["gcum"])
                    tt(gs["tmp"], pch["l"][:, 0:16], gs["gcum"], ALU.subtract)

                def c_egl(c2):
                    p_ = ps()
                    mm(p_[:, 0:16], CST[C_SC0 + c2], gs["gcum"])
                    act(gs["egl%d" % c2], p_[:, 0:16], AF.Exp)

                chain = [
                    lambda: ts(gs["tmp"], gs["y"], -1.0, ALU.mult),
                    lambda: tt(gs["ay"], gs["y"], gs["tmp"], ALU.max),
                    lambda: act(gs["e1"], gs["ay"], AF.Exp, scale=-1.0),
                    lambda: act(gs["l1"], gs["e1"], AF.Ln, bias=1.0),
                    lambda: ts(gs["m"], gs["y"], 0.0, ALU.max),
                    lambda: tt(gs["m"], gs["m"], gs["l1"], ALU.add),
                    lambda: tt(v3(gs["g"]), v3(gs["m"]), nA3, ALU.mult),
                    lambda: ts(gs["nbeta"], gs["beta"], -1.0, ALU.mult),
                    c_gcum,
                    lambda: ts(gs["ngcum"], gs["gcum"], -1.0, ALU.mult),
                    c_sell,
                    lambda: act(gs["sckd"], gs["tmp"], AF.Exp),
                    lambda: act(gs["egc"], gs["gcum"], AF.Exp),
                    lambda: tt(gs["scw"], gs["beta"], gs["egc"], ALU.mult),
                    lambda: c_egl(0),
                    lambda: c_egl(1),
                ]

                W = None
                sqs = {}
                for j in range(12):
                    if j % 2 == 0:
                        W = nextw(l, "a", j // 2)
                    p = proj(W, slice((j % 2) * 128, (j % 2) * 128 + 128), hT, KC)
                    pre = pres[pren[0] % 2]
                    pren[0] += 1
                    cp(pre[:, 0:3], halo_g[j][:, 0:3], eng="pool")
                    cp(pre[:, 3:515], p, eng="dve")
                    cp(halo_g[j][:, 0:3], pre[:, 512:515], eng="pool")
                    acc = big()
                    ts(acc, pre[:, 0:TB], ppc(l, PP_GCW + j * 4), ALU.mult)
                    for k in range(1, 4):
                        stt(acc, pre[:, k:k + TB], ppc(l, PP_GCW + j * 4 + k), acc, ALU.mult, ALU.add)
                    act(qkvT[j] if j < 8 else qkb[j], acc, AF.Silu)
                    if j < 8:
                        act(sqb[j], qkvT[j], AF.Square)
                    for _ in range(2):
                        if chain:
                            chain.pop(0)()
                while chain:
                    chain.pop(0)()
                for j in range(4):
                    if j % 2 == 0:
                        W = nextw(l, "a", 6 + j // 2)
                    p = proj(W, slice((j % 2) * 128, (j % 2) * 128 + 128), hT, KC)
                    act(zT[j], p, AF.Silu)

                def l2norm(j):
                    p_ = ps()
                    mm(p_, ones_bf, sqb[j])
                    rn = big()
                    rsqrt(rn, p_, 1.0, EPS)
                    stt(qkb[j], qkvT[j], (DK ** -0.5) if j < 4 else 1.0, rn, ALU.mult, ALU.mult)

                for c in range(4):
                    if c % 2 == 0:
                        W = nextw(l, "a", 8 + c // 2)
                    p = proj(W, slice((c % 2) * 128, (c % 2) * 128 + 128), hT, KC)
                    cp(gy[c], p)
                    l2norm(c)
                for c in range(4):
                    if c % 2 == 0:
                        W = nextw(l, "a", 10 + c // 2)
                    p = proj(W, slice((c % 2) * 128, (c % 2) * 128 + 128), hT, KC)
                    sg = big()
                    act(sg, p, AF.Sigmoid, bias=ppc(l, PP_GLUB + 4 + c))
                    stt(ubuf[c][:, 30:542], gy[c], ppc(l, PP_GLUB + c), sg, ALU.add, ALU.mult)
                    l2norm(4 + c)
                def col(name, t, h):
                    return Tl(gs[name].ap[:, t * 4 + h:t * 4 + h + 1], gs[name].b)

                def bc_mid(ap2d):
                    return ap2d.unsqueeze(1).to_broadcast([128, 4, 128])

                def bc_last(ap2d):
                    return ap2d.unsqueeze(2).to_broadcast([128, 4, 128])

                def p3(pt):
                    return Tl(pt.ap[:, 0:512].rearrange("p (h f) -> p h f", h=4), pt.b)

                def pre4(t):
                    tsl = slice(t * 128, (t + 1) * 128)
                    GHb = GHbS[t % 2]
                    gsl = lambda n: Tl(bc_last(gs[n].ap[:, t * 4:(t + 1) * 4]), gs[n].b)
                    gcum_b, nbeta_b, scw_b, sckd_b, beta_b = gsl("gcum"), gsl("nbeta"), gsl("scw"), gsl("sckd"), gsl("beta")
                    idb32 = Tl(bc_mid(ident32.ap), ident32.b)
                    idbbf = Tl(bc_mid(ident_bf.ap), ident_bf.b)
                    nls_b = Tl(bc_mid(CST[C_NLS].ap), CST[C_NLS].b)
                    nui_b = Tl(bc_mid(CST[C_NUI].ap), CST[C_NUI].b)
                    flat = lambda w_: Tl(w_.ap.rearrange("p h f -> p (h f)"), w_.b)
                    qd4 = Tl(ghb_t[:, (names_hb.index("qd")) * 4:][:, 0:4, :], [GHb["qd"][h_].b for h_ in range(4)])
                    qkt4 = Tl(ghb_t[:, (names_hb.index("QKT")) * 4:][:, 0:4, :], [GHb["QKT"][h_].b for h_ in range(4)])
                    kdec4 = Tl(ghb_t[:, (names_hb.index("kdec")) * 4:][:, 0:4, :], [GHb["kdec"][h_].b for h_ in range(4)])
                    q4 = Tl(qkb_t[:, 0:4, tsl], [qkb[h_].b for h_ in range(4)])
                    tt(W4["diagG"], idb32, gcum_b, ALU.mult)
                    pG = ps()
                    mm(pG, ones32, flat(W4["diagG"]))
                    stt(W4["t1"], p3(pG), -1.0, gcum_b, ALU.mult, ALU.add)
                    tt(W4["t2"], nui_b, W4["t1"], ALU.subtract)
                    tt(W4["t1"], W4["t1"], nls_b, ALU.add)
                    act(flat(W4["Dl"]), flat(W4["t1"]), AF.Exp)
                    act(flat(W4["Du"]), flat(W4["t2"]), AF.Exp)
                    act(flat(W4["eg"]), pG, AF.Exp)
                    tt(qd4, q4, W4["eg"], ALU.mult)
                    pK = ps()
                    for h in range(4):
                        kTn = qkb[4 + h][:, tsl]
                        mm(pK[:, h * 128:(h + 1) * 128], kTn, kTn)
                    tt(W4["t2"], p3(pK), W4["Dl"], ALU.mult)
                    tt(W4["Pa"], W4["t2"], nbeta_b, ALU.mult)
                    pT = psb()
                    for h in range(4):
                        tr(pT[:, h * 128:(h + 1) * 128], GT[h]["Pa"], ident_bf)
                    pT3 = Tl(pT.ap[:, 0:512].rearrange("p (h f) -> p h f", h=4), pT.b)
                    cp(Tl(QX4[0].ap[:, :, 0:128], QX4[0].b), pT3)
                    tt(Tl(QX4[0].ap[:, :, 128:256], QX4[0].b), pT3, idbbf, ALU.add)
                    pQ = ps()
                    for h in range(4):
                        mm(pQ[:, h * 128:(h + 1) * 128], qkb[4 + h][:, tsl], qkb[h][:, tsl])
                    tt(qkt4, p3(pQ), W4["Du"], ALU.mult)
                    pk = psb()
                    for h in range(4):
                        tr(pk[:, h * 128:(h + 1) * 128], qkb[4 + h][:, tsl], ident_bf)
                    pk3 = Tl(pk.ap[:, 0:512].rearrange("p (h f) -> p h f", h=4), pk.b)
                    tt(W4["kbg"], pk3, scw_b, ALU.mult)
                    tt(kdec4, pk3, sckd_b, ALU.mult)
                    pv = psb()
                    for h in range(4):
                        tr(pv[:, h * 128:(h + 1) * 128], qkb[8 + h][:, tsl], ident_bf)
                    pv3 = Tl(pv.ap[:, 0:512].rearrange("p (h f) -> p h f", h=4), pv.b)
                    tt(W4["vb"], pv3, beta_b, ALU.mult)
                    pP = ps()
                    pQ1 = ps()
                    for h in range(4):
                        mm(pP[:, h * 128:(h + 1) * 128], QX[h][0][:, 0:128], GT[h]["Pa"])
                        mm(pQ1[:, h * 128:(h + 1) * 128], GT[h]["Pa"], QX[h][0][:, 0:128])
                    cp(W4["Pb"], p3(pP))
                    cp(Tl(QX4[1].ap[:, :, 0:128], QX4[1].b), p3(pQ1))
                    cp(Tl(QX4[1].ap[:, :, 128:256], QX4[1].b), Tl(QX4[0].ap[:, :, 128:256], QX4[0].b), eng="dve")

                def neumann4(t):
                    GHb = GHbS[t % 2]
                    Gu = GuS[t % 2]
                    Pk4, Pn4 = W4["Pb"], W4["Pa"]
                    Pk = [GT[h_]["Pb"] for h_ in range(4)]
                    Pn = [GT[h_]["Pa"] for h_ in range(4)]
                    ik, in_ = 1, 0
                    Qv = lambda i_: Tl(QX4[i_].ap[:, :, 0:128], QX4[i_].b)
                    Xv = lambda i_: Tl(QX4[i_].ap[:, :, 128:256], QX4[i_].b)
                    for lev in range(1, 6):
                        if lev < 5:
                            pQn, pXn, pPn = ps(), ps(), ps()
                            for h in range(4):
                                hs = slice(h * 128, (h + 1) * 128)
                                mm(pQn[:, hs], Pk[h], QX[h][ik][:, 0:128])
                                mm(pXn[:, hs], Pk[h], QX[h][ik][:, 128:256])
                                mm(pPn[:, hs], QX[h][ik][:, 0:128], Pk[h])
                            cp(Qv(in_), p3(pQn))
                            tt(Xv(in_), Xv(ik), p3(pXn), ALU.add)
                            cp(Pn4, p3(pPn))
                            Pk4, Pn4 = Pn4, Pk4
                            Pk, Pn = Pn, Pk
                            ik, in_ = in_, ik
                        else:
                            pXn = ps()
                            for h in range(4):
                                mm(pXn[:, h * 128:(h + 1) * 128], Pk[h], QX[h][ik][:, 128:256])
                            tt(Xv(in_), Xv(ik), p3(pXn), ALU.add)
                            ik, in_ = in_, ik
                        yield
                    pu, pw = ps(), ps()
                    for h in range(4):
                        hs = slice(h * 128, (h + 1) * 128)
                        TT = QX[h][ik][:, 128:256]
                        mm(pu[:, hs], TT, GT[h]["vb"])
                        mm(pw[:, hs], GT[h]["kbg"], TT)
                    u4_ = Tl(gh_t[:, names_h.index("u") * 4:][:, 0:4, :], [Gu[h_].b for h_ in range(4)])
                    wT4 = Tl(ghb_t[:, names_hb.index("wT") * 4:][:, 0:4, :], [GHb["wT"][h_].b for h_ in range(4)])
                    cp(u4_, p3(pu))
                    cp(wT4, p3(pw), eng="dve")
                    yield

                def A_gen(t):
                    pre4(t)
                    yield
                    for _ in neumann4(t):
                        yield

                def R_gen(t):
                    tsl = slice(t * 128, (t + 1) * 128)
                    GHb = GHbS[t % 2]
                    Gu = GuS[t % 2]
                    hb = lambda n: [GHb[n][h_].b for h_ in range(4)]
                    vnew4 = Tl(ghb_t[:, names_hb.index("vnew") * 4:][:, 0:4, :], hb("vnew"))
                    on4 = Tl(ghb_t[:, names_hb.index("on") * 4:][:, 0:4, :], hb("on"))
                    u4 = Tl(gh_t[:, names_h.index("u") * 4:][:, 0:4, :], [Gu[h_].b for h_ in range(4)])
                    o4 = Tl(gh_t[:, names_h.index("o") * 4:][:, 0:4, :], [GH["o"][h_].b for h_ in range(4)])
                    osq4 = Tl(gh_t[:, names_h.index("on") * 4:][:, 0:4, :], [GH["on"][h_].b for h_ in range(4)])
                    S4 = Tl(S_t[:], [S[h_].b for h_ in range(4)])
                    Sb4 = Tl(Sb_t[:], [Sb[h_].b for h_ in range(4)])
                    for c2 in range(2):
                        r = slice(c2 * 64, (c2 + 1) * 64)
                        egl_b = Tl(bc_last(gs["egl%d" % c2].ap[:, t * 4:(t + 1) * 4]), gs["egl%d" % c2].b)
                        p1 = ps()
                        for h in range(4):
                            mm(p1[:, h * 128:(h + 1) * 128], GHb["wT"][h], Sb[h])
                        tt(vnew4[r], u4[r], p3(p1)[r], ALU.subtract)
                        tt(S4, S4, egl_b, ALU.mult)
                        yield
                        p2 = ps()
                        p3_ = ps()
                        for h in range(4):
                            mm(p2[:, h * 128:(h + 1) * 128], GHb["qd"][h], Sb[h], start=True, stop=False)
                            mm(p2[:, h * 128:(h + 1) * 128], GHb["QKT"][h][r, :], GHb["vnew"][h][r, :], start=False, stop=True)
                            mm(p3_[:, h * 128:(h + 1) * 128], GHb["kdec"][h][r, :], GHb["vnew"][h][r, :])
                        tt(Sb4, S4, p3(p3_), ALU.add)
                        tt(S4, S4, p3(p3_), ALU.add)
                        cp(o4[r], p3(p2)[r])
                        yield
                    yield "norm"
                    tt(osq4, o4, o4, ALU.mult)
                    kb.op("dve", lambda e: e.tensor_reduce(out=ss.ap, in_=osq4.ap, axis=mybir.AxisListType.X, op=ALU.add),
                          reads=osq4.b, writes=[ss.b])
                    act(ssd, ss, AF.Sqrt, bias=EPS, scale=1.0 / 128)
                    recip(srs, ssd)
                    tt(on4, o4, Tl(bc_last(ss_t[:, 8:12]), srs.b), ALU.mult)
                    yield
                    po = psb()
                    for h in range(4):
                        tr(po[:, h * 128:(h + 1) * 128], GHb["on"][h], ident_bf)
                    po3 = Tl(po.ap[:, 0:512].rearrange("p (h f) -> p h f", h=4), po.b)
                    stt(Tl(oa_t[:, 0:4, tsl], [oaT[h_].b for h_ in range(4)]), po3, ppc(l, PP_GN),
                        Tl(zT_t[:, 0:4, tsl], [zT[h_].b for h_ in range(4)]), ALU.mult, ALU.mult)
                    yield

                for _ in A_gen(0):
                    pass
                for t in range(4):
                    R = R_gen(t)
                    for tag in R:
                        if tag == "norm":
                            break
                    A = A_gen(t + 1) if t < 3 else None
                    if A is not None:
                        next(A)
                    for _ in R:
                        pass
                    if A is not None:
                        for _ in A:
                            pass

                if l == 0 and b == 0:
                    dbg("hT", Tl(hT_t[:].rearrange("p c t -> p (c t)"), hT[0].b), [128, KC * TB])
                    dbg("qkvT", Tl(qkv_t[:].rearrange("p c t -> p (c t)"), qkvT[0].b), [128, 12 * TB])
                    dbg("oaT", Tl(oa_t[:].rearrange("p c t -> p (c t)"), oaT[0].b), [128, 4 * TB])
                    dbg("gs", Tl(gs_t[:].rearrange("p c t -> p (c t)"), gs["g"].b), [128, len(gs_names) * 16])
                for c in range(4):
                    pc = ps()
                    for k in range(31):
                        dgk = diag(l, PP_CCW + c * 31 + k)
                        mm(pc, dgk, ubuf[c][:, k:k + TB], start=(k == 0), stop=(k == 30))
                    act(gy[c], pc, AF.Identity, bias=ppc(l, PP_CCB + c))
                    cp(ubuf[c][:, 0:30], ubuf[c][:, 512:542], eng="pool")
                pm = ps()
                for c in range(4):
                    mm(pm, ones32, gy[c], start=(c == 0), stop=(c == 3))
                pq = ps()
                for c in range(4):
                    sq = bigb()
                    act(sq, gy[c], AF.Square)
                    mm(pq, ones_bf, sq, start=(c == 0), stop=(c == 3))
                mean = gy[4]
                act(mean, pm, AF.Identity, scale=1.0 / 512)
                msq = gy[5]
                tt(msq, mean, mean, ALU.mult)
                var = gy[6]
                stt(var, pq, 1.0 / 512, msq, ALU.mult, ALU.subtract)
                rstd = gy[7]
                rsqrt(rstd, var, 1.0, EPS)
                for c in range(4):
                    dlt = big()
                    tt(dlt, gy[c], mean, ALU.subtract)
                    tt(dlt, dlt, rstd, ALU.mult)
                    act(ucT[c], dlt, AF.Silu, bias=ppc(l, PP_LNB + c), scale=ppc(l, PP_LNW + c))

                for h in range(4):
                    if h % 2 == 0:
                        W = nextw(l, "a", 12 + h // 2)
                    p = proj(W, slice((h % 2) * 128, (h % 2) * 128 + 128), hT, KC)
                    cp(qcT[h], p)
                for h in range(4):
                    for mt in range(2):
                        psc = ps()
                        mm(psc, kmT[h][:, mt * 128:(mt + 1) * 128], qcT[h])
                        act(Ebuf[mt], psc, AF.Exp, scale=DK ** -0.5)
                    po = ps()
                    pd = ps()
                    for mt in range(2):
                        mm(po, vm[mt][:, h * 128:(h + 1) * 128], Ebuf[mt], start=(mt == 0), stop=(mt == 1))
                        mm(pd, ones_bf, Ebuf[mt], start=(mt == 0), stop=(mt == 1))
                    rden = big()
                    act(rden, pd, AF.Ln)
                    act(rden, rden, AF.Exp, scale=-1.0)
                    tt(ocT[h], po, rden, ALU.mult)

                if l == 0 and b == 0:
                    dbg("ucT", Tl(uc_t[:].rearrange("p c t -> p (c t)"), ucT[0].b), [128, 4 * TB], ucT)
                    dbg("ocT", Tl(oc_t[:].rearrange("p c t -> p (c t)"), ocT[0].b), [128, 4 * TB], ocT)
                for m in range(4):
                    srcs = [oaT, ucT, ocT]
                    for i in range(3):
                        W = nextw(l, "a", 26 + 4 * i + m)
                        Wv = Tl(W.ap.rearrange("p k n -> p (k n)")[:, 0:1024].rearrange("p (k n) -> p k n", k=4), W.b)
                        for jj in range(2):
                            p = proj(Wv, slice(jj * 128, jj * 128 + 128), srcs[i], 4)
                            cp(gy[i * 2 + jj], p)
                    for i in range(3):
                        W = nextw(l, "a", 14 + 4 * i + m)
                        for jj in range(2):
                            j = 2 * m + jj
                            p = proj(W, slice(jj * 128, jj * 128 + 128), hT, KC)
                            gt_ = big()
                            act(gt_, p, AF.Sigmoid, bias=ppc(l, PP_GATEB + i * 8 + j))
                            if i == 0:
                                tt(gy[6 + jj], gy[jj], gt_, ALU.mult)
                            elif i == 1:
                                tt(gt_, gy[2 + jj], gt_, ALU.mult)
                                tt(gy[6 + jj], gy[6 + jj], gt_, ALU.add)
                            else:
                                tt(gt_, gy[4 + jj], gt_, ALU.mult)
                                tt(mgT[j], gy[6 + jj], gt_, ALU.add)
                for m in range(4):
                    W = nextw(l, "a", 38 + m)
                    for jj in range(2):
                        o = 2 * m + jj
                        p = proj(W, slice(jj * 128, jj * 128 + 128), mgT, KC)
                        tt(xT[o], xT[o], p, ALU.add)

                if l == 0 and b == 0:
                    dbg("mgT", Tl(mg_t[:, 0:KC, :].rearrange("p c t -> p (c t)"), mgT[0].b), [128, KC * TB], mgT)
                    dbg("xmid", Tl(xT_t[:].rearrange("p c t -> p (c t)"), xT[0].b), [128, KC * TB], xT)
                rmsnorm(l, PP_NFFN, hT)
                for fg in range(11):
                    Wg = nextw(l, "a", 42 + fg)
                    sgs = []
                    for jj in range(2):
                        f = 2 * fg + jj
                        cols = slice(jj * 128, jj * 128 + 128)
                        p = proj(Wg, cols, hT, KC)
                        pre = pres[pren[0] % 2]
                        pren[0] += 1
                        cp(pre[:, 0:2], halo_f[f][:, 0:2], eng="pool")
                        cp(pre[:, 2:514], p)
                        cp(halo_f[f][:, 0:2], pre[:, 512:514], eng="pool")
                        sg = big()
                        ts(sg, pre[:, 0:TB], ppc(l, PP_FFW + f * 3), ALU.mult)
                        for k in range(1, 3):
                            stt(sg, pre[:, k:k + TB], ppc(l, PP_FFW + f * 3 + k), sg, ALU.mult, ALU.add)
                        act(sg, sg, AF.Silu, bias=ppc(l, PP_FFB + f))
                        sgs.append(sg)
                    Wv = nextw(l, "a", 53 + fg)
                    for jj in range(2):
                        f = 2 * fg + jj
                        cols = slice(jj * 128, jj * 128 + 128)
                        pv = proj(Wv, cols, hT, KC)
                        tt(actf[f], pv, sgs[jj], ALU.mult)
                if l == 0 and b == 0:
                    dbg("hT2", Tl(hT_t[:].rearrange("p c t -> p (c t)"), hT[0].b), [128, KC * TB], hT)
                    dbg("actf", Tl(af_t[:].rearrange("p c t -> p (c t)"), actf[0].b), [128, NF * TB], actf)
                for o in range(8):
                    Wd = nextw(l, "d", o)
                    p = ps()
                    for f in range(NF):
                        mm(p, Wd[:, f, :], actf[f], start=(f == 0), stop=(f == NF - 1))
                    tt(xT[o], xT[o], p, ALU.add)

                if l == 0 and b == 0:
                    dbg("xend", Tl(xT_t[:].rearrange("p c t -> p (c t)"), xT[0].b), [128, KC * TB], xT)
                if l == NL - 1:
                    rmsnorm(l, PP_NFIN, qkvT[0:8])
                    kb.dma("sp", xs[l + 1][b], qkv_t[:, 0:8, :].rearrange("p c t -> p (c t)"),
                           reads=[t.b for t in qkvT[0:8]], writes=[xs_b[l + 1][b]])
                else:
                    kb.dma("sp", xs[l + 1][b], xT_t[:].rearrange("p c t -> p (c t)"),
                           reads=[t.b for t in xT], writes=[xs_b[l + 1][b]])

        kb.finish(xs_b[NL] + list(dbg_out.values()))
        print("instructions:", kb.n_instr, "sems:", len(kb.sems))
    return nc


def _consts():
    idx = np.arange(128)
    ch = idx // 64
    same = ch[:, None] == ch[None, :]
    c = np.zeros((NCST, 128, 128), np.float32)
    c[C_ID] = np.eye(128)
    c[C_ONE] = 1.0
    c[C_MLE] = (same & (idx[:, None] <= idx[None, :]))
    c[C_NLS] = np.where(same & (idx[:, None] > idx[None, :]), 0.0, NEGBIG)
    c[C_NUI] = np.where(same & (idx[:, None] <= idx[None, :]), 0.0, NEGBIG)
    c[C_SELL] = (idx[:, None] == (ch[None, :] * 64 + 63))
    c[C_SC0] = (idx[:, None] == 63) & np.ones((1, 128), bool)
    c[C_SC1] = (idx[:, None] == 127) & np.ones((1, 128), bool)
    return np.ascontiguousarray(c.transpose(1, 0, 2).reshape(128, NCST * 128))


def _grp(w, ncols=256):
    K, N = w.shape
    kc = K // 128
    g = w.reshape(kc, 128, N // ncols, ncols).transpose(2, 1, 0, 3).reshape(N // ncols, 128, kc * ncols)
    return g


def _fm(v, n):
    return np.asarray(v, np.float32).reshape(n, 128).T


def _prep_layer(inp, l):
    w_in = np.asarray(inp["w_in"][l], np.float32)
    cols = np.r_[0:1536, 1544:2056, 2056:3080, 3080:3592, 3592:6664]
    wmain = w_in[:, cols]
    groups = [_grp(wmain)]
    for nm in ("w_gdn_out", "w_cc_out", "w_xa_out"):
        g = _grp(np.asarray(inp[nm][l], np.float32))
        groups.append(np.concatenate([g, np.zeros_like(g)], axis=2))
    groups.append(_grp(np.asarray(inp["w_o"][l], np.float32)))
    groups.append(_grp(np.asarray(inp["w_up"][l], np.float32)))
    groups.append(_grp(np.asarray(inp["w_mem_kv"][l], np.float32)))
    wa = np.ascontiguousarray(np.concatenate(groups, axis=0))
    assert wa.shape == (NGA, 128, 2048), wa.shape
    wd = np.ascontiguousarray(_grp(np.asarray(inp["w_down"][l], np.float32), 128))
    wab = np.ascontiguousarray(w_in[:, 1536:1544].reshape(8, 128, 8).transpose(1, 0, 2).reshape(128, 64))
    pp = np.zeros((128, NPP), np.float32)
    pp[:, PP_NMIX:PP_NMIX + 8] = _fm(inp["norm_mix"][l], 8)
    gcw = np.asarray(inp["gdn_conv_w"][l], np.float32)
    pp[:, PP_GCW:PP_GCW + 48] = gcw.reshape(4, 12, 128).transpose(2, 1, 0).reshape(128, 48)
    pp[:, PP_GLUB:PP_GLUB + 8] = _fm(inp["cc_glu_b"][l], 8)
    ccw = np.asarray(inp["cc_dw_w"][l], np.float32)
    pp[:, PP_CCW:PP_CCW + 124] = ccw.reshape(31, 4, 128).transpose(2, 1, 0).reshape(128, 124)
    pp[:, PP_CCB:PP_CCB + 4] = _fm(inp["cc_dw_b"][l], 4)
    pp[:, PP_LNW:PP_LNW + 4] = _fm(inp["cc_ln_w"][l], 4)
    pp[:, PP_LNB:PP_LNB + 4] = _fm(inp["cc_ln_b"][l], 4)
    pp[:, PP_MEMN:PP_MEMN + 8] = _fm(inp["mem_norm"][l], 8)
    pp[:, PP_GATEB:PP_GATEB + 24] = _fm(inp["gate_b"][l], 24)
    pp[:, PP_NFFN:PP_NFFN + 8] = _fm(inp["norm_ffn"][l], 8)
    ffw = np.asarray(inp["ffn_dw_w"][l], np.float32)
    pp[:, PP_FFW:PP_FFW + 66] = ffw.reshape(3, 22, 128).transpose(2, 1, 0).reshape(128, 66)
    pp[:, PP_FFB:PP_FFB + 22] = _fm(inp["ffn_dw_b"][l], 22)
    pp[:, PP_GN] = np.asarray(inp["gdn_norm"][l], np.float32)
    pp[:, PP_NFIN:PP_NFIN + 8] = _fm(inp["norm_final"], 8)
    pb = np.zeros((128, 8), np.float32)
    pb[:, 0:4] = np.asarray(inp["gdn_dt_bias"][l], np.float32)[None, :]
    pb[:, 4:8] = np.asarray(inp["gdn_a_log"][l], np.float32)[None, :]
    return {f"wa{l}": wa, f"wd{l}": wd, f"wab{l}": wab, f"pp{l}": pp, f"pb{l}": pb}


def _x_to_fm(xb):
    T = xb.shape[0]
    return np.ascontiguousarray(xb.reshape(T // TB, TB, KC, 128).transpose(0, 3, 2, 1).reshape(T // TB, 128, KC * TB))


def _fm_to_x(o, T):
    return o.reshape(T // TB, 128, KC, TB).transpose(0, 3, 2, 1).reshape(T, D)


def run(inputs, n_batch=None, T=None, NL=None, dbg_names=(), trace=False):
    x = np.asarray(inputs["x"], np.float32)
    mem = np.asarray(inputs["mem"], np.float32)
    Bn, Sn, _ = x.shape
    T = Sn if T is None else T
    NL = inputs["w_in"].shape[0] if NL is None else NL
    nc = build(T, NL, dbg_names)
    shared = {"cst": _consts()}
    for l in range(NL):
        shared.update(_prep_layer(inputs, l))
    in_maps = []
    for b in range(Bn):
        m = dict(shared)
        m["x"] = _x_to_fm(x[b, :T])
        m["mem"] = np.ascontiguousarray(mem[b].reshape(2, 128, D))
        in_maps.append(m)
    res = run_bass_kernel_spmd(nc, in_maps, core_ids=list(range(Bn)), trace=trace)
    out = np.stack([_fm_to_x(np.asarray(r["out"]), T) for r in res.results], axis=0)
    return out, res


def kernel(**inputs):
    out, _ = run(inputs)
    return out.astype(np.float32)
```

```python
import numpy as np
from contextlib import ExitStack
import concourse.bass as bass
import concourse.mybir as mybir
from concourse.bass_utils import run_bass_kernel_spmd

F32 = mybir.dt.float32
BF16 = mybir.dt.bfloat16
AF = mybir.ActivationFunctionType
ALU = mybir.AluOpType

D = 1024
KC = 8
TB = 512
EPS = 1e-6
DK = 128
FFN = 2816
NF = 22
NGA = 68
WSLOT = 2816
NEGBIG = -30000.0

PP_NMIX = 0
PP_GCW = 8
PP_GLUB = 56
PP_CCW = 64
PP_CCB = 188
PP_LNW = 192
PP_LNB = 196
PP_MEMN = 200
PP_GATEB = 208
PP_NFFN = 232
PP_FFW = 240
PP_FFB = 306
PP_GN = 328
PP_NFIN = 329
NPP = 337

C_ID, C_ONE, C_MLE, C_NLS, C_NUI, C_SELL, C_SC0, C_SC1 = range(8)
NCST = 8

import os
INTERLEAVE = os.environ.get('K_INTER', '1') == '1'
PIPE = os.environ.get('K_PIPE', '0') == '1'
A_PER_R = int(os.environ.get('K_APR', '6'))
YMASK = int(os.environ.get('K_YMASK', '1984'))
SEM_EPOCH = 16000
N_DMA_SEMS = 24


class Buf:
    __slots__ = ("lw", "rd")

    def __init__(self):
        self.lw = None
        self.rd = {}


class Tl:
    __slots__ = ("ap", "b")

    def __init__(self, ap, b=None):
        self.ap = ap
        self.b = b if b is not None else Buf()

    def __getitem__(self, k):
        return Tl(self.ap[k], self.b)


class KB:
    def __init__(self, nc, stack):
        self.nc = nc
        self.stack = stack
        self.eng = {"pe": nc.tensor, "act": nc.scalar, "dve": nc.vector, "pool": nc.gpsimd, "sp": nc.sync}
        self.sems = {}
        self.cur = {}
        self.epoch = {}
        for e in self.eng:
            self.epoch[e] = 0
            self._new_epoch(e)
        self.seen = {e: {} for e in self.eng}
        self.dsem = []
        for i in range(N_DMA_SEMS):
            k = f"d{i}"
            self.sems[k] = stack.enter_context(nc.semaphore(k))
            self.dsem.append([k, 0])
        self.dnext = 0
        self.qsem = []
        for i in range(8):
            k = f"q{i}"
            self.sems[k] = stack.enter_context(nc.semaphore(k))
            self.qsem.append([k, 0])
        self.qnext = 0
        self.n_instr = 0

    def _new_epoch(self, e):
        k = f"{e}_{self.epoch[e]}"
        self.sems[k] = self.stack.enter_context(self.nc.semaphore(k))
        self.cur[e] = [k, 0]
        self.epoch[e] += 1

    def _wait(self, e, ev):
        k, v = ev
        if self.seen[e].get(k, 0) >= v:
            return
        if e == "pe" and k.startswith("pe_"):
            return
        self.eng[e].wait_ge(self.sems[k], v)
        self.seen[e][k] = v

    @staticmethod
    def _flat(bs):
        o = []
        for b in bs:
            if isinstance(b, (list, tuple)):
                o.extend(b)
            else:
                o.append(b)
        return o

    def _deps(self, e, reads, writes):
        reads = self._flat(reads)
        writes = self._flat(writes)
        for b in reads:
            if b.lw is not None:
                self._wait(e, b.lw)
        for b in writes:
            if b.lw is not None:
                self._wait(e, b.lw)
            for k, v in b.rd.items():
                self._wait(e, (k, v))

    def _mark(self, ev, reads, writes):
        reads = self._flat(reads)
        writes = self._flat(writes)
        for b in writes:
            b.lw = ev
            b.rd = {}
        for b in reads:
            b.rd[ev[0]] = ev[1]

    def op(self, e, fn, reads=(), writes=()):
        self._deps(e, reads, writes)
        ins = fn(self.eng[e])
        c = self.cur[e]
        c[1] += 1
        ins.then_inc(self.sems[c[0]], 1)
        self._mark((c[0], c[1]), reads, writes)
        self.n_instr += 1
        if c[1] >= SEM_EPOCH:
            self._new_epoch(e)

    def dma(self, q, out, in_, reads=(), writes=()):
        if q == "pool":
            d = self.qsem[self.qnext]
            self.qnext = (self.qnext + 1) % len(self.qsem)
        else:
            d = self.dsem[self.dnext]
            self.dnext = (self.dnext + 1) % len(self.dsem)
        if d[1] > 0:
            self._wait(q, (d[0], d[1]))
        self._deps(q, reads, writes)
        ins = self.eng[q].dma_start(out=out, in_=in_)
        d[1] += 16
        ins.then_inc(self.sems[d[0]], 16)
        self._mark((d[0], d[1]), reads, writes)
        self.n_instr += 1

    def finish(self, bufs):
        for b in bufs:
            if b.lw is not None:
                self._wait("sp", b.lw)


def build(T, NL, dbg_names=()):
    NBLK = T // TB
    nc = bass.Bass("TRN2", target_bir_lowering=False)
    x_in = nc.dram_tensor("x", [NBLK, 128, KC * TB], F32, kind="ExternalInput").ap()
    mem_in = nc.dram_tensor("mem", [2, 128, D], F32, kind="ExternalInput").ap()
    cst_in = nc.dram_tensor("cst", [128, NCST * 128], F32, kind="ExternalInput").ap()
    wa_in = [nc.dram_tensor(f"wa{l}", [NGA, 128, 2048], F32, kind="ExternalInput").ap() for l in range(NL)]
    wd_in = [nc.dram_tensor(f"wd{l}", [8, 128, WSLOT], F32, kind="ExternalInput").ap() for l in range(NL)]
    pp_in = [nc.dram_tensor(f"pp{l}", [128, NPP], F32, kind="ExternalInput").ap() for l in range(NL)]
    pb_in = [nc.dram_tensor(f"pb{l}", [128, 8], F32, kind="ExternalInput").ap() for l in range(NL)]
    wab_in = [nc.dram_tensor(f"wab{l}", [128, 64], F32, kind="ExternalInput").ap() for l in range(NL)]
    out_d = nc.dram_tensor("out", [NBLK, 128, KC * TB], F32, kind="ExternalOutput").ap()
    wabf = [nc.dram_tensor(f"wabf{l}", [NGA, 128, 2048], BF16).ap() for l in range(NL)]
    wdbf = [nc.dram_tensor(f"wdbf{l}", [8, 128, WSLOT], BF16).ap() for l in range(NL)]
    xs = [x_in] + [nc.dram_tensor(f"xs{l}", [NBLK, 128, KC * TB], F32).ap() for l in range(1, NL)] + [out_d]
    xs_b = [[Buf() for _ in range(NBLK)] for _ in range(NL + 1)]
    dbg_out = {}

    with ExitStack() as st:
        kb = KB(nc, st)

        def sbt(name, shape, dt):
            return st.enter_context(nc.sbuf_tensor("sb_" + name, shape, dt))

        def rb(*tl):
            return [t.b for t in tl if isinstance(t, Tl)]

        def apof(v):
            return v.ap if isinstance(v, Tl) else v

        def mm(ps, lhsT, rhs, start=True, stop=True):
            kb.op("pe", lambda e: e.matmul(ps.ap, lhsT.ap, rhs.ap, start=start, stop=stop),
                  reads=[lhsT.b, rhs.b], writes=[ps.b])

        def tr(ps, in_, idn):
            kb.op("pe", lambda e: e.transpose(ps.ap, in_.ap, idn.ap), reads=[in_.b, idn.b], writes=[ps.b])

        def act(out, in_, func, bias=None, scale=None, accum=None, eng="act"):
            kw = {}
            if bias is not None:
                kw["bias"] = apof(bias)
            if scale is not None:
                kw["scale"] = apof(scale)
            wr = [out.b]
            if accum is not None:
                kw["accum_out"] = accum.ap
                wr.append(accum.b)
            kb.op("act", lambda e: e.activation(out=out.ap, in_=in_.ap, func=func, **kw),
                  reads=[in_.b] + rb(bias, scale), writes=wr)

        def tt(out, a, b, op, eng="dve"):
            kb.op(eng, lambda e: e.tensor_tensor(out=out.ap, in0=a.ap, in1=b.ap, op=op), reads=[a.b, b.b], writes=[out.b])

        def ts(out, a, s1, op0, s2=None, op1=None, eng="dve"):
            if op1 is None:
                kb.op(eng, lambda e: e.tensor_scalar(out=out.ap, in0=a.ap, scalar1=apof(s1), scalar2=None, op0=op0),
                      reads=[a.b] + rb(s1), writes=[out.b])
            else:
                kb.op(eng, lambda e: e.tensor_scalar(out=out.ap, in0=a.ap, scalar1=apof(s1), scalar2=apof(s2), op0=op0, op1=op1),
                      reads=[a.b] + rb(s1, s2), writes=[out.b])

        def stt(out, in0, scalar, in1, op0, op1):
            kb.op("dve", lambda e: e.scalar_tensor_tensor(out=out.ap, in0=in0.ap, scalar=apof(scalar), in1=in1.ap, op0=op0, op1=op1),
                  reads=[in0.b, in1.b] + rb(scalar), writes=[out.b])

        def rsqrt(out, in_, scale, bias):
            act(out, in_, AF.Ln, bias=bias, scale=scale)
            act(out, out, AF.Exp, scale=-0.5)

        def recip(out, in_):
            kb.op("dve", lambda e: e.reciprocal(out=out.ap, in_=in_.ap), reads=[in_.b], writes=[out.b])

        def cp(out, in_, eng="act"):
            if eng == "act":
                kb.op("act", lambda e: e.copy(out=out.ap, in_=in_.ap), reads=[in_.b], writes=[out.b])
            else:
                kb.op(eng, lambda e: e.tensor_copy(out=out.ap, in_=in_.ap), reads=[in_.b], writes=[out.b])

        def memset(t, v, eng="dve"):
            kb.op(eng, lambda e: e.memset(t.ap, v), writes=[t.b])

        def dbg(name, t, shape, extra=()):
            if name not in dbg_names:
                return
            t = Tl(t.ap, t.b)
            for e_ in extra:
                kb._deps("pool", [e_.b], [])
            d = nc.dram_tensor("dbg_" + name, list(shape), F32, kind="ExternalOutput").ap()
            tmp = Buf()
            kb.dma("pool", d, t.ap, reads=[t.b], writes=[tmp])
            dbg_out[name] = tmp

        cst_t = sbt("cst", [128, NCST * 128], F32)
        cstb = Buf()
        kb.dma("sp", cst_t[:], cst_in, writes=[cstb])
        CST = [Tl(cst_t[:, i * 128:(i + 1) * 128], cstb) for i in range(NCST)]
        ident32, ones32 = CST[C_ID], CST[C_ONE]
        cbf_t = sbt("cbf", [128, 256], BF16)
        ident_bf = Tl(cbf_t[:, 0:128])
        ones_bf = Tl(cbf_t[:, 128:256])
        cp(ident_bf, ident32)
        cp(ones_bf, ones32)

        psum_tiles = [Tl(st.enter_context(nc.psum_tensor(f"ps{i}", [128, 512], F32))[:]) for i in range(8)]
        psn = [0]

        def ps():
            t = psum_tiles[psn[0] % 8]
            psn[0] += 1
            return t

        def psb():
            t = ps()
            return Tl(t.ap.bitcast(BF16), t.b)

        xT_t = sbt("xT", [128, KC, TB], F32)
        xT = [Tl(xT_t[:, c, :]) for c in range(KC)]
        xT_all = Tl(xT_t[:].rearrange("p c t -> p (c t)"))
        hT_t = sbt("hT", [128, KC, TB], BF16)
        hT = [Tl(hT_t[:, c, :]) for c in range(KC)]
        qkv_t = sbt("qkvT", [128, 8, TB], F32)
        qkvT = [Tl(qkv_t[:, j, :]) for j in range(8)]
        qkb_t = sbt("qkb", [128, 12, TB], BF16)
        qkb = [Tl(qkb_t[:, j, :]) for j in range(12)]
        zT_t = sbt("zT", [128, 4, TB], BF16)
        zT = [Tl(zT_t[:, j, :]) for j in range(4)]
        oa_t = sbt("oaT", [128, 4, TB], BF16)
        oaT = [Tl(oa_t[:, j, :]) for j in range(4)]
        uc_t = sbt("ucT", [128, 4, TB], BF16)
        ucT = [Tl(uc_t[:, j, :]) for j in range(4)]
        oc_t = sbt("ocT", [128, 4, TB], BF16)
        ocT = [Tl(oc_t[:, j, :]) for j in range(4)]
        qcT = zT
        af_t = sbt("actf", [128, NF, TB], BF16)
        actf = [Tl(af_t[:, j, :]) for j in range(NF)]
        mgT = actf[0:KC]
        mg_t = af_t
        big_t = sbt("big", [128, 6, TB], F32)
        bigs = [Tl(big_t[:, j, :]) for j in range(6)]
        bign = [0]

        def big():
            t = bigs[bign[0] % 6]
            bign[0] += 1
            return t

        def bigb():
            t = big()
            return Tl(t.ap.bitcast(BF16)[:, 0:TB], t.b)

        pre_t = sbt("pre", [128, 2, 544], BF16)
        pres = [Tl(pre_t[:, j, :]) for j in range(2)]
        pren = [0]
        ub_t = sbt("ubuf", [128, 4, 544], BF16)
        ubuf = [Tl(ub_t[:, j, :]) for j in range(4)]
        hg_t = sbt("halo_g", [128, 12, 4], BF16)
        halo_g = [Tl(hg_t[:, j, :]) for j in range(12)]
        hf_t = sbt("halo_f", [128, NF, 2], BF16)
        halo_f = [Tl(hf_t[:, j, :]) for j in range(NF)]
        dg_t = sbt("diag", [128, 16, 128], BF16)
        dgs = [Tl(dg_t[:, j, :]) for j in range(16)]
        dgn = [0]
        w_t = sbt("wring", [128, 3, WSLOT], BF16)
        wslots = [Tl(w_t[:, j, :]) for j in range(3)]
        S_t = sbt("S", [128, 4, 128], F32)
        S = [Tl(S_t[:, h, :]) for h in range(4)]
        E_t = sbt("E", [128, 4, TB], BF16)
        Ebuf = [Tl(E_t[:, j, :]) for j in range(4)]
        km_t = sbt("kmT", [128, 4, 256], BF16)
        kmT = [Tl(km_t[:, h, :]) for h in range(4)]
        vm_t = sbt("vm", [128, 2, 512], BF16)
        vm = [Tl(vm_t[:, j, :]) for j in range(2)]
        mn_t = sbt("memnT", [128, KC, 256], BF16)
        memnT = [Tl(mn_t[:, c, :]) for c in range(KC)]
        pp_t = [sbt(f"ppT{l}", [128, NPP], F32) for l in range(NL)]
        ppT = [Tl(pp_t[l][:]) for l in range(NL)]
        pb_t = [sbt(f"pbT{l}", [128, 8], F32) for l in range(NL)]
        pbT = [Tl(pb_t[l][:]) for l in range(NL)]
        wab_t = [sbt(f"wabT{l}", [128, KC, 8], BF16) for l in range(NL)]
        wabT = [Tl(wab_t[l][:]) for l in range(NL)]
        negA_t = sbt("negA", [128, 4], F32)
        negA = Tl(negA_t[:])
        gs_names = ["y", "ay", "e1", "l1", "m", "g", "beta", "nbeta", "gcum", "ngcum", "sckd", "egc", "scw", "egl0", "egl1", "tmp"]
        gs_t = sbt("gs", [128, len(gs_names), 16], F32)
        gs = {n: Tl(gs_t[:, i, :]) for i, n in enumerate(gs_names)}
        names_h = ["u", "wT", "qd", "QKT", "kdec", "vnew", "o", "on"]
        gh_t = sbt("gh", [128, len(names_h) * 4, 128], F32)
        GH = {n: [Tl(gh_t[:, i * 4 + h, :]) for h in range(4)] for i, n in enumerate(names_h)}
        gy = [Tl(gh_t[:, 4 * j:4 * j + 4, :].rearrange("p a t -> p (a t)"),
                 [GH[names_h[j]][h].b for h in range(4)]) for j in range(8)]
        names_t = ["diagG", "t1", "Dl", "t2", "Du", "eg"]
        names_tb = ["kbg", "vb", "Pa", "Pb"]
        gtb_t = sbt("gtb", [128, 4 * len(names_tb), 128], BF16)
        names_hb = ["wT", "qd", "QKT", "kdec", "vnew", "on"]
        ghb_t = sbt("ghb", [128, len(names_hb) * 4, 128], BF16)
        sqb_t = sbt("sqb", [128, 8, TB], BF16)
        sqb = [Tl(sqb_t[:, j, :]) for j in range(8)]
        sqd_t = sbt("sqdummy", [128, 128], BF16)
        sqdummy = Tl(sqd_t[:])
        Sb_t = sbt("Sb", [128, 4, 128], BF16)
        Sb = [Tl(Sb_t[:, h, :]) for h in range(4)]
        gt_t = sbt("gt", [128, 4 * len(names_t), 128], F32)
        GT = [{n: Tl(gt_t[:, i * 4 + s, :]) for i, n in enumerate(names_t)} for s in range(4)]
        for s_ in range(4):
            for i, n in enumerate(names_tb):
                GT[s_][n] = Tl(gtb_t[:, i * 4 + s_, :])

        def wide(tensor, idx0, tls):
            return Tl(tensor[:, idx0:idx0 + 4, :], [t_.b for t_ in tls])
        W4 = {n: wide(gt_t, i * 4, [GT[h_][n] for h_ in range(4)]) for i, n in enumerate(names_t)}
        for i, n in enumerate(names_tb):
            W4[n] = wide(gtb_t, i * 4, [GT[h_][n] for h_ in range(4)])
        _ghb0 = {n: [Tl(ghb_t[:, i * 4 + h, :]) for h in range(4)] for i, n in enumerate(names_hb)}
        GHbS = [_ghb0, _ghb0]
        GuS = [GH["u"], GH["u"]]
        qx_t = sbt("qx", [128, 2, 4, 256], BF16)
        QX = [[Tl(qx_t[:, i, s, :]) for i in range(2)] for s in range(4)]
        QX4 = [Tl(qx_t[:, i, :, :], [QX[h_][i].b for h_ in range(4)]) for i in range(2)]
        on4_t = sbt("ones4", [128, 4, 128], F32)
        ones4 = [Tl(on4_t[:, h, :]) for h in range(4)]
        for h in range(4):
            memset(ones4[h], 1.0)
        ss_t = sbt("ss", [128, 16], F32)
        ss = Tl(ss_t[:, 0:4]); ssd = Tl(ss_t[:, 4:8]); srs = Tl(ss_t[:, 8:12])
        mss = Tl(ss_t[:, 12:13]); msd = Tl(ss_t[:, 13:14]); mrs = Tl(ss_t[:, 14:15])

        wa_b = [[Buf() for _ in range(NGA)] for _ in range(NL)]
        wd_b = [[Buf() for _ in range(8)] for _ in range(NL)]
        blk_seq = [("a", g) for g in range(14)]
        for m in range(4):
            blk_seq += [("a", 26 + m), ("a", 30 + m), ("a", 34 + m), ("a", 14 + m), ("a", 18 + m), ("a", 22 + m)]
        blk_seq += [("a", 38 + m) for m in range(4)]
        for fg in range(11):
            blk_seq += [("a", 42 + fg), ("a", 53 + fg)]
        blk_seq += [("d", o) for o in range(8)]
        lay_seq = [("a", 64), ("a", 65), ("a", 66), ("a", 67)]
        allseq = []
        for l in range(NL):
            allseq += [(l,) + s for s in lay_seq]
            for b in range(NBLK):
                allseq += [(l,) + s for s in blk_seq]
        done = set()
        for (l, kind, g) in allseq:
            if (l, kind, g) in done:
                continue
            done.add((l, kind, g))
            if kind == "a":
                kb.dma("pool", wabf[l][g], wa_in[l][g], writes=[wa_b[l][g]])
            else:
                kb.dma("pool", wdbf[l][g], wd_in[l][g], writes=[wd_b[l][g]])

        wstate = {"issued": 0, "next": 0}
        LOOK = 2

        def nextw(l, kind, g):
            i = wstate["next"]
            assert allseq[i] == (l, kind, g), (allseq[i], (l, kind, g))
            while wstate["issued"] < min(i + 1 + LOOK, len(allseq)):
                j = wstate["issued"]
                (l2, k2, g2) = allseq[j]
                slot = wslots[j % 3]
                if k2 == "a":
                    kb.dma("sp", slot.ap[:, 0:2048], wabf[l2][g2], reads=[wa_b[l2][g2]], writes=[slot.b])
                else:
                    kb.dma("sp", slot.ap, wdbf[l2][g2], reads=[wd_b[l2][g2]], writes=[slot.b])
                wstate["issued"] += 1
            wstate["next"] += 1
            slot = wslots[i % 3]
            if kind == "a":
                return Tl(slot.ap[:, 0:2048].rearrange("p (k n) -> p k n", k=KC), slot.b)
            return Tl(slot.ap.rearrange("p (f n) -> p f n", f=NF), slot.b)

        def ppc(l, col, n=1):
            return Tl(pp_t[l][:, col:col + n], ppT[l].b)

        def diag(l, col):
            d = dgs[dgn[0] % 16]
            dgn[0] += 1
            if dgn[0] % 2 == 0:
                kb.op("dve", lambda e: e.tensor_scalar(out=d.ap, in0=ident_bf.ap, scalar1=pp_t[l][:, col:col + 1], scalar2=None, op0=ALU.mult),
                      reads=[ident_bf.b, ppT[l].b], writes=[d.b])
            else:
                kb.op("act", lambda e: e.activation(out=d.ap, in_=ident_bf.ap, func=AF.Identity, scale=pp_t[l][:, col:col + 1]),
                      reads=[ident_bf.b, ppT[l].b], writes=[d.b])
            return d

        def proj(W, cols, src, nk):
            p = ps()
            for kc in range(nk):
                mm(p, W[:, kc, cols], src[kc], start=(kc == 0), stop=(kc == nk - 1))
            return p

        def rmsnorm(l, wcol, outs):
            pss = ps()
            for c in range(KC):
                sq = bigb()
                act(sq, xT[c], AF.Square)
                mm(pss, ones_bf, sq, start=(c == 0), stop=(c == KC - 1))
            rstd = big()
            rsqrt(rstd, pss, 1.0 / D, EPS)
            for c in range(KC):
                stt(outs[c], xT[c], ppc(l, wcol + c), rstd, ALU.mult, ALU.mult)

        for l in range(NL):
            kb.dma("sp", pp_t[l][:], pp_in[l], writes=[ppT[l].b])
            kb.dma("sp", pb_t[l][:], pb_in[l], writes=[pbT[l].b])
            kb.dma("pool", wab_t[l][:].rearrange("p k n -> p (k n)"), wab_in[l], writes=[wabT[l].b])
            act(negA, Tl(pb_t[l][:, 4:8], pbT[l].b), AF.Exp)
            ts(negA, negA, -1.0, ALU.mult)
            for h in range(4):
                memset(S[h], 0.0)
                memset(Sb[h], 0.0)
            for j in range(12):
                memset(halo_g[j], 0.0, eng="pool")
            for f in range(NF):
                memset(halo_f[f], 0.0, eng="pool")
            for c in range(4):
                memset(ubuf[c][:, 0:32], 0.0, eng="pool")
            memt = [Tl(qkv_t[:, 2 * mt:2 * mt + 2, :].rearrange("p a t -> p (a t)"), qkvT[2 * mt].b) for mt in range(2)]
            for mt in range(2):
                kb.dma("sp", memt[mt].ap, mem_in[mt], writes=[memt[mt].b, qkvT[2 * mt + 1].b])
            for mt in range(2):
                sq = big()
                for hh in range(2):
                    act(sq, memt[mt][:, hh * 512:(hh + 1) * 512], AF.Square, accum=Tl(ss_t[:, 12 + hh:13 + hh], mss.b))
                tt(mss, Tl(ss_t[:, 12:13], mss.b), Tl(ss_t[:, 13:14], mss.b), ALU.add)
                act(msd, mss, AF.Sqrt, bias=EPS, scale=1.0 / D)
                recip(mrs, msd)
                for hh in range(2):
                    act(memt[mt][:, hh * 512:(hh + 1) * 512], memt[mt][:, hh * 512:(hh + 1) * 512], AF.Identity, scale=mrs)
            for c in range(KC):
                p = ps()
                for mt in range(2):
                    tr(p[:, mt * 128:(mt + 1) * 128], memt[mt][:, c * 128:(c + 1) * 128], ident32)
                ts(memnT[c], p[:, 0:256], ppc(l, PP_MEMN + c), ALU.mult)
            for g in range(2):
                W = nextw(l, "a", 64 + g)
                for hh in range(2):
                    h = 2 * g + hh
                    p = ps()
                    for kc in range(KC):
                        mm(p[:, 0:256], W[:, kc, hh * 128:(hh + 1) * 128], memnT[kc], start=(kc == 0), stop=(kc == KC - 1))
                    cp(kmT[h], p[:, 0:256])
            for g in range(2):
                W = nextw(l, "a", 66 + g)
                for mt in range(2):
                    p = ps()
                    for kc in range(KC):
                        mm(p[:, 0:256], memnT[kc][:, mt * 128:(mt + 1) * 128], W[:, kc, :], start=(kc == 0), stop=(kc == KC - 1))
                    cp(vm[mt][:, g * 256:(g + 1) * 256], p[:, 0:256])

            for b in range(NBLK):
                kb.dma("sp", xT_t[:].rearrange("p c t -> p (c t)"), xs[l][b], reads=[xs_b[l][b]], writes=[t.b for t in xT])
                rmsnorm(l, PP_NMIX, hT)

                pab = ps()
                for t in range(4):
                    for kc in range(KC):
                        mm(pab[:, t * 8:(t + 1) * 8], hT[kc][:, t * 128:(t + 1) * 128], wabT[l][:, kc, :], start=(kc == 0), stop=(kc == KC - 1))
                pab3 = Tl(pab.ap[:, 0:32].rearrange("p (t e) -> p t e", t=4), pab.b)
                v3 = lambda t_: Tl(t_.ap.rearrange("p (t e) -> p t e", t=4), t_.b)
                dtb = Tl(pb_t[l][:, 0:4].unsqueeze(1).to_broadcast([128, 4, 4]), pbT[l].b)
                nA3 = Tl(negA_t[:, 0:4].unsqueeze(1).to_broadcast([128, 4, 4]), negA.b)
                tt(v3(gs["y"]), pab3[:, :, 0:4], dtb, ALU.add)
                act(v3(gs["beta"]), pab3[:, :, 4:8], AF.Sigmoid)
                pch = {}

                def c_gcum():
                    pch["g"] = ps()
                    mm(pch["g"][:, 0:16], CST[C_MLE], gs["g"])
                    cp(gs["gcum"], pch["g"][:, 0:16])

                def c_sell():
                    pch["l"] = ps()
                    mm(pch["l"][:, 0:16], CST[C_SELL], gs["gcum"])
                    tt(gs["tmp"], pch["l"][:, 0:16], gs["gcum"], ALU.subtract)

                def c_egl(c2):
                    p_ = ps()
                    mm(p_[:, 0:16], CST[C_SC0 + c2], gs["gcum"])
                    act(gs["egl%d" % c2], p_[:, 0:16], AF.Exp)

                chain = [
                    lambda: ts(gs["tmp"], gs["y"], -1.0, ALU.mult),
                    lambda: tt(gs["ay"], gs["y"], gs["tmp"], ALU.max),
                    lambda: act(gs["e1"], gs["ay"], AF.Exp, scale=-1.0),
                    lambda: act(gs["l1"], gs["e1"], AF.Ln, bias=1.0),
                    lambda: ts(gs["m"], gs["y"], 0.0, ALU.max),
                    lambda: tt(gs["m"], gs["m"], gs["l1"], ALU.add),
                    lambda: tt(v3(gs["g"]), v3(gs["m"]), nA3, ALU.mult),
                    lambda: ts(gs["nbeta"], gs["beta"], -1.0, ALU.mult),
                    c_gcum,
                    lambda: ts(gs["ngcum"], gs["gcum"], -1.0, ALU.mult),
                    c_sell,
                    lambda: act(gs["sckd"], gs["tmp"], AF.Exp),
                    lambda: act(gs["egc"], gs["gcum"], AF.Exp),
                    lambda: tt(gs["scw"], gs["beta"], gs["egc"], ALU.mult),
                    lambda: c_egl(0),
                    lambda: c_egl(1),
                ]

                W = None
                sqs = {}
                for j in range(12):
                    if j % 2 == 0:
                        W = nextw(l, "a", j // 2)
                    p = proj(W, slice((j % 2) * 128, (j % 2) * 128 + 128), hT, KC)
                    pre = pres[pren[0] % 2]
                    pren[0] += 1
                    cp(pre[:, 0:3], halo_g[j][:, 0:3], eng="pool")
                    cp(pre[:, 3:515], p, eng="dve")
                    cp(halo_g[j][:, 0:3], pre[:, 512:515], eng="pool")
                    acc = big()
                    ts(acc, pre[:, 0:TB], ppc(l, PP_GCW + j * 4), ALU.mult)
                    for k in range(1, 4):
                        stt(acc, pre[:, k:k + TB], ppc(l, PP_GCW + j * 4 + k), acc, ALU.mult, ALU.add)
                    act(qkvT[j] if j < 8 else qkb[j], acc, AF.Silu)
                    if j < 8:
                        act(sqb[j], qkvT[j], AF.Square)
                    for _ in range(2):
                        if chain:
                            chain.pop(0)()
                while chain:
                    chain.pop(0)()
                for j in range(4):
                    if j % 2 == 0:
                        W = nextw(l, "a", 6 + j // 2)
                    p = proj(W, slice((j % 2) * 128, (j % 2) * 128 + 128), hT, KC)
                    act(zT[j], p, AF.Silu)

                def l2norm(j):
                    p_ = ps()
                    mm(p_, ones_bf, sqb[j])
                    rn = big()
                    rsqrt(rn, p_, 1.0, EPS)
                    stt(qkb[j], qkvT[j], (DK ** -0.5) if j < 4 else 1.0, rn, ALU.mult, ALU.mult)

                for c in range(4):
                    if c % 2 == 0:
                        W = nextw(l, "a", 8 + c // 2)
                    p = proj(W, slice((c % 2) * 128, (c % 2) * 128 + 128), hT, KC)
                    cp(gy[c], p)
                    l2norm(c)
                for c in range(4):
                    if c % 2 == 0:
                        W = nextw(l, "a", 10 + c // 2)
                    p = proj(W, slice((c % 2) * 128, (c % 2) * 128 + 128), hT, KC)
                    sg = big()
                    act(sg, p, AF.Sigmoid, bias=ppc(l, PP_GLUB + 4 + c))
                    stt(ubuf[c][:, 30:542], gy[c], ppc(l, PP_GLUB + c), sg, ALU.add, ALU.mult)
                    l2norm(4 + c)
                def col(name, t, h):
                    return Tl(gs[name].ap[:, t * 4 + h:t * 4 + h + 1], gs[name].b)

                def bc_mid(ap2d):
                    return ap2d.unsqueeze(1).to_broadcast([128, 4, 128])

                def bc_last(ap2d):
                    return ap2d.unsqueeze(2).to_broadcast([128, 4, 128])

                def p3(pt):
                    return Tl(pt.ap[:, 0:512].rearrange("p (h f) -> p h f", h=4), pt.b)

                def pre4(t):
                    tsl = slice(t * 128, (t + 1) * 128)
                    GHb = GHbS[t % 2]
                    gsl = lambda n: Tl(bc_last(gs[n].ap[:, t * 4:(t + 1) * 4]), gs[n].b)
                    gcum_b, nbeta_b, scw_b, sckd_b, beta_b = gsl("gcum"), gsl("nbeta"), gsl("scw"), gsl("sckd"), gsl("beta")
                    idb32 = Tl(bc_mid(ident32.ap), ident32.b)
                    idbbf = Tl(bc_mid(ident_bf.ap), ident_bf.b)
                    nls_b = Tl(bc_mid(CST[C_NLS].ap), CST[C_NLS].b)
                    nui_b = Tl(bc_mid(CST[C_NUI].ap), CST[C_NUI].b)
                    flat = lambda w_: Tl(w_.ap.rearrange("p h f -> p (h f)"), w_.b)
                    qd4 = Tl(ghb_t[:, (names_hb.index("qd")) * 4:][:, 0:4, :], [GHb["qd"][h_].b for h_ in range(4)])
                    qkt4 = Tl(ghb_t[:, (names_hb.index("QKT")) * 4:][:, 0:4, :], [GHb["QKT"][h_].b for h_ in range(4)])
                    kdec4 = Tl(ghb_t[:, (names_hb.index("kdec")) * 4:][:, 0:4, :], [GHb["kdec"][h_].b for h_ in range(4)])
                    q4 = Tl(qkb_t[:, 0:4, tsl], [qkb[h_].b for h_ in range(4)])
                    tt(W4["diagG"], idb32, gcum_b, ALU.mult)
                    pG = ps()
                    mm(pG, ones32, flat(W4["diagG"]))
                    stt(W4["t1"], p3(pG), -1.0, gcum_b, ALU.mult, ALU.add)
                    tt(W4["t2"], nui_b, W4["t1"], ALU.subtract)
                    tt(W4["t1"], W4["t1"], nls_b, ALU.add)
                    act(flat(W4["Dl"]), flat(W4["t1"]), AF.Exp)
                    act(flat(W4["Du"]), flat(W4["t2"]), AF.Exp)
                    act(flat(W4["eg"]), pG, AF.Exp)
                    tt(qd4, q4, W4["eg"], ALU.mult)
                    pK = ps()
                    for h in range(4):
                        kTn = qkb[4 + h][:, tsl]
                        mm(pK[:, h * 128:(h + 1) * 128], kTn, kTn)
                    tt(W4["t2"], p3(pK), W4["Dl"], ALU.mult)
                    tt(W4["Pa"], W4["t2"], nbeta_b, ALU.mult)
                    pT = psb()
                    for h in range(4):
                        tr(pT[:, h * 128:(h + 1) * 128], GT[h]["Pa"], ident_bf)
                    pT3 = Tl(pT.ap[:, 0:512].rearrange("p (h f) -> p h f", h=4), pT.b)
                    cp(Tl(QX4[0].ap[:, :, 0:128], QX4[0].b), pT3)
                    tt(Tl(QX4[0].ap[:, :, 128:256], QX4[0].b), pT3, idbbf, ALU.add)
                    pQ = ps()
                    for h in range(4):
                        mm(pQ[:, h * 128:(h + 1) * 128], qkb[4 + h][:, tsl], qkb[h][:, tsl])
                    tt(qkt4, p3(pQ), W4["Du"], ALU.mult)
                    pk = psb()
                    for h in range(4):
                        tr(pk[:, h * 128:(h + 1) * 128], qkb[4 + h][:, tsl], ident_bf)
                    pk3 = Tl(pk.ap[:, 0:512].rearrange("p (h f) -> p h f", h=4), pk.b)
                    tt(W4["kbg"], pk3, scw_b, ALU.mult)
                    tt(kdec4, pk3, sckd_b, ALU.mult)
                    pv = psb()
                    for h in range(4):
                        tr(pv[:, h * 128:(h + 1) * 128], qkb[8 + h][:, tsl], ident_bf)
                    pv3 = Tl(pv.ap[:, 0:512].rearrange("p (h f) -> p h f", h=4), pv.b)
                    tt(W4["vb"], pv3, beta_b, ALU.mult)
                    pP = ps()
                    pQ1 = ps()
                    for h in range(4):
                        mm(pP[:, h * 128:(h + 1) * 128], QX[h][0][:, 0:128], GT[h]["Pa"])
                        mm(pQ1[:, h * 128:(h + 1) * 128], GT[h]["Pa"], QX[h][0][:, 0:128])
                    cp(W4["Pb"], p3(pP))
                    cp(Tl(QX4[1].ap[:, :, 0:128], QX4[1].b), p3(pQ1))
                    cp(Tl(QX4[1].ap[:, :, 128:256], QX4[1].b), Tl(QX4[0].ap[:, :, 128:256], QX4[0].b), eng="dve")

                def head_gen(h, t, tsl):
                    GHb = GHbS[t % 2]
                    Gu = GuS[t % 2]
                    G = GT[h]
                    Pk, Pn = G["Pb"], G["Pa"]
                    qk_, qn_ = QX[h][1], QX[h][0]
                    for lev in range(1, 6):
                        pX = ps()
                        if lev < 5:
                            mm(pX[:, 0:256], Pk, qk_[:, 0:256])
                            pP = ps()
                            mm(pP[:, 0:128], qk_[:, 0:128], Pk)
                            cp(qn_[:, 0:128], pX[:, 0:128])
                            tt(qn_[:, 128:256], qk_[:, 128:256], pX[:, 128:256], ALU.add)
                            cp(Pn, pP[:, 0:128])
                            Pk, Pn = Pn, Pk
                            qk_, qn_ = qn_, qk_
                            yield "neu"
                        else:
                            mm(pX[:, 0:128], Pk, qk_[:, 128:256])
                            tt(qn_[:, 128:256], qk_[:, 128:256], pX[:, 0:128], ALU.add)
                            qk_, qn_ = qn_, qk_
                            yield "neu"
                    TT = qk_[:, 128:256]
                    pu = ps()
                    mm(pu[:, 0:128], TT, G["vb"])
                    cp(Gu[h], pu[:, 0:128])
                    yield "neu"
                    pw = ps()
                    mm(pw[:, 0:128], G["kbg"], TT)
                    cp(GHb["wT"][h], pw[:, 0:128], eng="dve")
                    yield "neu"

                def A_gen(t):
                    tsl = slice(t * 128, (t + 1) * 128)
                    pre4(t)
                    yield
                    gens = [head_gen(h, t, tsl) for h in range(4)]
                    while gens:
                        for g_ in list(gens):
                            try:
                                next(g_)
                                yield
                            except StopIteration:
                                gens.remove(g_)

                def R_gen(t):
                    tsl = slice(t * 128, (t + 1) * 128)
                    GHb = GHbS[t % 2]
                    Gu = GuS[t % 2]
                    hb = lambda n: [GHb[n][h_].b for h_ in range(4)]
                    vnew4 = Tl(ghb_t[:, names_hb.index("vnew") * 4:][:, 0:4, :], hb("vnew"))
                    on4 = Tl(ghb_t[:, names_hb.index("on") * 4:][:, 0:4, :], hb("on"))
                    u4 = Tl(gh_t[:, names_h.index("u") * 4:][:, 0:4, :], [Gu[h_].b for h_ in range(4)])
                    o4 = Tl(gh_t[:, names_h.index("o") * 4:][:, 0:4, :], [GH["o"][h_].b for h_ in range(4)])
                    osq4 = Tl(gh_t[:, names_h.index("on") * 4:][:, 0:4, :], [GH["on"][h_].b for h_ in range(4)])
                    S4 = Tl(S_t[:], [S[h_].b for h_ in range(4)])
                    Sb4 = Tl(Sb_t[:], [Sb[h_].b for h_ in range(4)])
                    for c2 in range(2):
                        r = slice(c2 * 64, (c2 + 1) * 64)
                        egl_b = Tl(bc_last(gs["egl%d" % c2].ap[:, t * 4:(t + 1) * 4]), gs["egl%d" % c2].b)
                        p1 = ps()
                        for h in range(4):
                            mm(p1[:, h * 128:(h + 1) * 128], GHb["wT"][h], Sb[h])
                        tt(vnew4[r], u4[r], p3(p1)[r], ALU.subtract)
                        tt(S4, S4, egl_b, ALU.mult)
                        yield
                        p2 = ps()
                        p3_ = ps()
                        for h in range(4):
                            mm(p2[:, h * 128:(h + 1) * 128], GHb["qd"][h], Sb[h], start=True, stop=False)
                            mm(p2[:, h * 128:(h + 1) * 128], GHb["QKT"][h][r, :], GHb["vnew"][h][r, :], start=False, stop=True)
                            mm(p3_[:, h * 128:(h + 1) * 128], GHb["kdec"][h][r, :], GHb["vnew"][h][r, :])
                        tt(Sb4, S4, p3(p3_), ALU.add)
                        tt(S4, S4, p3(p3_), ALU.add)
                        cp(o4[r], p3(p2)[r])
                        yield
                    yield "norm"
                    tt(osq4, o4, o4, ALU.mult)
                    kb.op("dve", lambda e: e.tensor_reduce(out=ss.ap, in_=osq4.ap, axis=mybir.AxisListType.X, op=ALU.add),
                          reads=osq4.b, writes=[ss.b])
                    act(ssd, ss, AF.Sqrt, bias=EPS, scale=1.0 / 128)
                    recip(srs, ssd)
                    tt(on4, o4, Tl(bc_last(ss_t[:, 8:12]), srs.b), ALU.mult)
                    yield
                    po = psb()
                    for h in range(4):
                        tr(po[:, h * 128:(h + 1) * 128], GHb["on"][h], ident_bf)
                    po3 = Tl(po.ap[:, 0:512].rearrange("p (h f) -> p h f", h=4), po.b)
                    stt(Tl(oa_t[:, 0:4, tsl], [oaT[h_].b for h_ in range(4)]), po3, ppc(l, PP_GN),
                        Tl(zT_t[:, 0:4, tsl], [zT[h_].b for h_ in range(4)]), ALU.mult, ALU.mult)
                    yield

                for _ in A_gen(0):
                    pass
                for t in range(4):
                    R = R_gen(t)
                    for tag in R:
                        if tag == "norm":
                            break
                    A = A_gen(t + 1) if t < 3 else None
                    if A is not None:
                        next(A)
                    for _ in R:
                        pass
                    if A is not None:
                        for _ in A:
                            pass

                if l == 0 and b == 0:
                    dbg("hT", Tl(hT_t[:].rearrange("p c t -> p (c t)"), hT[0].b), [128, KC * TB])
                    dbg("qkvT", Tl(qkv_t[:].rearrange("p c t -> p (c t)"), qkvT[0].b), [128, 12 * TB])
                    dbg("oaT", Tl(oa_t[:].rearrange("p c t -> p (c t)"), oaT[0].b), [128, 4 * TB])
                    dbg("gs", Tl(gs_t[:].rearrange("p c t -> p (c t)"), gs["g"].b), [128, len(gs_names) * 16])
                for c in range(4):
                    pc = ps()
                    for k in range(31):
                        dgk = diag(l, PP_CCW + c * 31 + k)
                        mm(pc, dgk, ubuf[c][:, k:k + TB], start=(k == 0), stop=(k == 30))
                    act(gy[c], pc, AF.Identity, bias=ppc(l, PP_CCB + c))
                    cp(ubuf[c][:, 0:30], ubuf[c][:, 512:542], eng="pool")
                pm = ps()
                for c in range(4):
                    mm(pm, ones32, gy[c], start=(c == 0), stop=(c == 3))
                pq = ps()
                for c in range(4):
                    sq = bigb()
                    act(sq, gy[c], AF.Square)
                    mm(pq, ones_bf, sq, start=(c == 0), stop=(c == 3))
                for h in range(4):
                    if h % 2 == 0:
                        W = nextw(l, "a", 12 + h // 2)
                    p = proj(W, slice((h % 2) * 128, (h % 2) * 128 + 128), hT, KC)
                    cp(qcT[h], p)
                mean = gy[4]
                act(mean, pm, AF.Identity, scale=1.0 / 512)
                msq = gy[5]
                tt(msq, mean, mean, ALU.mult)
                var = gy[6]
                stt(var, pq, 1.0 / 512, msq, ALU.mult, ALU.subtract)
                rstd = gy[7]
                rsqrt(rstd, var, 1.0, EPS)
                for c in range(4):
                    dlt = big()
                    tt(dlt, gy[c], mean, ALU.subtract)
                    tt(dlt, dlt, rstd, ALU.mult)
                    act(ucT[c], dlt, AF.Silu, bias=ppc(l, PP_LNB + c), scale=ppc(l, PP_LNW + c))

                def xa_scores(h):
                    for mt in range(2):
                        psc = ps()
                        mm(psc, kmT[h][:, mt * 128:(mt + 1) * 128], qcT[h])
                        act(Ebuf[(h % 2) * 2 + mt], psc, AF.Exp, scale=DK ** -0.5)

                def xa_pv(h):
                    po = ps()
                    pd = ps()
                    for mt in range(2):
                        E_ = Ebuf[(h % 2) * 2 + mt]
                        mm(po, vm[mt][:, h * 128:(h + 1) * 128], E_, start=(mt == 0), stop=(mt == 1))
                        mm(pd, ones_bf, E_, start=(mt == 0), stop=(mt == 1))
                    rden = big()
                    act(rden, pd, AF.Ln)
                    act(rden, rden, AF.Exp, scale=-1.0)
                    tt(ocT[h], po, rden, ALU.mult)

                xa_scores(0)
                for h in range(4):
                    if h + 1 < 4:
                        xa_scores(h + 1)
                    xa_pv(h)

                if l == 0 and b == 0:
                    dbg("ucT", Tl(uc_t[:].rearrange("p c t -> p (c t)"), ucT[0].b), [128, 4 * TB], ucT)
                    dbg("ocT", Tl(oc_t[:].rearrange("p c t -> p (c t)"), ocT[0].b), [128, 4 * TB], ocT)
                for m in range(4):
                    srcs = [oaT, ucT, ocT]
                    for i in range(3):
                        W = nextw(l, "a", 26 + 4 * i + m)
                        Wv = Tl(W.ap.rearrange("p k n -> p (k n)")[:, 0:1024].rearrange("p (k n) -> p k n", k=4), W.b)
                        for jj in range(2):
                            p = proj(Wv, slice(jj * 128, jj * 128 + 128), srcs[i], 4)
                            cp(gy[i * 2 + jj], p)
                    for i in range(3):
                        W = nextw(l, "a", 14 + 4 * i + m)
                        for jj in range(2):
                            j = 2 * m + jj
                            p = proj(W, slice(jj * 128, jj * 128 + 128), hT, KC)
                            gt_ = big()
                            act(gt_, p, AF.Sigmoid, bias=ppc(l, PP_GATEB + i * 8 + j))
                            if i == 0:
                                tt(gy[6 + jj], gy[jj], gt_, ALU.mult)
                            elif i == 1:
                                tt(gt_, gy[2 + jj], gt_, ALU.mult)
                                tt(gy[6 + jj], gy[6 + jj], gt_, ALU.add)
                            else:
                                tt(gt_, gy[4 + jj], gt_, ALU.mult)
                                tt(mgT[j], gy[6 + jj], gt_, ALU.add)
                for m in range(4):
                    W = nextw(l, "a", 38 + m)
                    for jj in range(2):
                        o = 2 * m + jj
                        p = proj(W, slice(jj * 128, jj * 128 + 128), mgT, KC)
                        tt(xT[o], xT[o], p, ALU.add)

                if l == 0 and b == 0:
                    dbg("mgT", Tl(mg_t[:, 0:KC, :].rearrange("p c t -> p (c t)"), mgT[0].b), [128, KC * TB], mgT)
                    dbg("xmid", Tl(xT_t[:].rearrange("p c t -> p (c t)"), xT[0].b), [128, KC * TB], xT)
                rmsnorm(l, PP_NFFN, hT)
                for fg in range(11):
                    Wg = nextw(l, "a", 42 + fg)
                    sgs = []
                    for jj in range(2):
                        f = 2 * fg + jj
                        cols = slice(jj * 128, jj * 128 + 128)
                        p = proj(Wg, cols, hT, KC)
                        pre = pres[pren[0] % 2]
                        pren[0] += 1
                        cp(pre[:, 0:2], halo_f[f][:, 0:2], eng="pool")
                        cp(pre[:, 2:514], p)
                        cp(halo_f[f][:, 0:2], pre[:, 512:514], eng="pool")
                        sg = big()
                        ts(sg, pre[:, 0:TB], ppc(l, PP_FFW + f * 3), ALU.mult)
                        for k in range(1, 3):
                            stt(sg, pre[:, k:k + TB], ppc(l, PP_FFW + f * 3 + k), sg, ALU.mult, ALU.add)
                        act(sg, sg, AF.Silu, bias=ppc(l, PP_FFB + f))
                        sgs.append(sg)
                    Wv = nextw(l, "a", 53 + fg)
                    for jj in range(2):
                        f = 2 * fg + jj
                        cols = slice(jj * 128, jj * 128 + 128)
                        pv = proj(Wv, cols, hT, KC)
                        tt(actf[f], pv, sgs[jj], ALU.mult)
                if l == 0 and b == 0:
                    dbg("hT2", Tl(hT_t[:].rearrange("p c t -> p (c t)"), hT[0].b), [128, KC * TB], hT)
                    dbg("actf", Tl(af_t[:].rearrange("p c t -> p (c t)"), actf[0].b), [128, NF * TB], actf)
                for o in range(8):
                    Wd = nextw(l, "d", o)
                    p = ps()
                    for f in range(NF):
                        mm(p, Wd[:, f, :], actf[f], start=(f == 0), stop=(f == NF - 1))
                    tt(xT[o], xT[o], p, ALU.add)

                if l == 0 and b == 0:
                    dbg("xend", Tl(xT_t[:].rearrange("p c t -> p (c t)"), xT[0].b), [128, KC * TB], xT)
                if l == NL - 1:
                    rmsnorm(l, PP_NFIN, qkvT[0:8])
                    kb.dma("sp", xs[l + 1][b], qkv_t[:, 0:8, :].rearrange("p c t -> p (c t)"),
                           reads=[t.b for t in qkvT[0:8]], writes=[xs_b[l + 1][b]])
                else:
                    kb.dma("sp", xs[l + 1][b], xT_t[:].rearrange("p c t -> p (c t)"),
                           reads=[t.b for t in xT], writes=[xs_b[l + 1][b]])

        kb.finish(xs_b[NL] + list(dbg_out.values()))
        print("instructions:", kb.n_instr, "sems:", len(kb.sems))
    return nc


def _consts():
    idx = np.arange(128)
    ch = idx // 64
    same = ch[:, None] == ch[None, :]
    c = np.zeros((NCST, 128, 128), np.float32)
    c[C_ID] = np.eye(128)
    c[C_ONE] = 1.0
    c[C_MLE] = (same & (idx[:, None] <= idx[None, :]))
    c[C_NLS] = np.where(same & (idx[:, None] > idx[None, :]), 0.0, NEGBIG)
    c[C_NUI] = np.where(same & (idx[:, None] <= idx[None, :]), 0.0, NEGBIG)
    c[C_SELL] = (idx[:, None] == (ch[None, :] * 64 + 63))
    c[C_SC0] = (idx[:, None] == 63) & np.ones((1, 128), bool)
    c[C_SC1] = (idx[:, None] == 127) & np.ones((1, 128), bool)
    return np.ascontiguousarray(c.transpose(1, 0, 2).reshape(128, NCST * 128))


def _grp(w, ncols=256):
    K, N = w.shape
    kc = K // 128
    g = w.reshape(kc, 128, N // ncols, ncols).transpose(2, 1, 0, 3).reshape(N // ncols, 128, kc * ncols)
    return g


def _fm(v, n):
    return np.asarray(v, np.float32).reshape(n, 128).T


def _prep_layer(inp, l):
    w_in = np.asarray(inp["w_in"][l], np.float32)
    cols = np.r_[0:1536, 1544:2056, 2056:3080, 3080:3592, 3592:6664]
    wmain = w_in[:, cols]
    groups = [_grp(wmain)]
    for nm in ("w_gdn_out", "w_cc_out", "w_xa_out"):
        g = _grp(np.asarray(inp[nm][l], np.float32))
        groups.append(np.concatenate([g, np.zeros_like(g)], axis=2))
    groups.append(_grp(np.asarray(inp["w_o"][l], np.float32)))
    groups.append(_grp(np.asarray(inp["w_up"][l], np.float32)))
    groups.append(_grp(np.asarray(inp["w_mem_kv"][l], np.float32)))
    wa = np.ascontiguousarray(np.concatenate(groups, axis=0))
    assert wa.shape == (NGA, 128, 2048), wa.shape
    wd = np.ascontiguousarray(_grp(np.asarray(inp["w_down"][l], np.float32), 128))
    wab = np.ascontiguousarray(w_in[:, 1536:1544].reshape(8, 128, 8).transpose(1, 0, 2).reshape(128, 64))
    pp = np.zeros((128, NPP), np.float32)
    pp[:, PP_NMIX:PP_NMIX + 8] = _fm(inp["norm_mix"][l], 8)
    gcw = np.asarray(inp["gdn_conv_w"][l], np.float32)
    pp[:, PP_GCW:PP_GCW + 48] = gcw.reshape(4, 12, 128).transpose(2, 1, 0).reshape(128, 48)
    pp[:, PP_GLUB:PP_GLUB + 8] = _fm(inp["cc_glu_b"][l], 8)
    ccw = np.asarray(inp["cc_dw_w"][l], np.float32)
    pp[:, PP_CCW:PP_CCW + 124] = ccw.reshape(31, 4, 128).transpose(2, 1, 0).reshape(128, 124)
    pp[:, PP_CCB:PP_CCB + 4] = _fm(inp["cc_dw_b"][l], 4)
    pp[:, PP_LNW:PP_LNW + 4] = _fm(inp["cc_ln_w"][l], 4)
    pp[:, PP_LNB:PP_LNB + 4] = _fm(inp["cc_ln_b"][l], 4)
    pp[:, PP_MEMN:PP_MEMN + 8] = _fm(inp["mem_norm"][l], 8)
    pp[:, PP_GATEB:PP_GATEB + 24] = _fm(inp["gate_b"][l], 24)
    pp[:, PP_NFFN:PP_NFFN + 8] = _fm(inp["norm_ffn"][l], 8)
    ffw = np.asarray(inp["ffn_dw_w"][l], np.float32)
    pp[:, PP_FFW:PP_FFW + 66] = ffw.reshape(3, 22, 128).transpose(2, 1, 0).reshape(128, 66)
    pp[:, PP_FFB:PP_FFB + 22] = _fm(inp["ffn_dw_b"][l], 22)
    pp[:, PP_GN] = np.asarray(inp["gdn_norm"][l], np.float32)
    pp[:, PP_NFIN:PP_NFIN + 8] = _fm(inp["norm_final"], 8)
    pb = np.zeros((128, 8), np.float32)
    pb[:, 0:4] = np.asarray(inp["gdn_dt_bias"][l], np.float32)[None, :]
    pb[:, 4:8] = np.asarray(inp["gdn_a_log"][l], np.float32)[None, :]
    return {f"wa{l}": wa, f"wd{l}": wd, f"wab{l}": wab, f"pp{l}": pp, f"pb{l}": pb}


def _x_to_fm(xb):
    T = xb.shape[0]
    return np.ascontiguousarray(xb.reshape(T // TB, TB, KC, 128).transpose(0, 3, 2, 1).reshape(T // TB, 128, KC * TB))


def _fm_to_x(o, T):
    return o.reshape(T // TB, 128, KC, TB).transpose(0, 3, 2, 1).reshape(T, D)


def run(inputs, n_batch=None, T=None, NL=None, dbg_names=(), trace=False):
    x = np.asarray(inputs["x"], np.float32)
    mem = np.asarray(inputs["mem"], np.float32)
    Bn, Sn, _ = x.shape
    T = Sn if T is None else T
    NL = inputs["w_in"].shape[0] if NL is None else NL
    nc = build(T, NL, dbg_names)
    shared = {"cst": _consts()}
    for l in range(NL):
        shared.update(_prep_layer(inputs, l))
    in_maps = []
    for b in range(Bn):
        m = dict(shared)
        m["x"] = _x_to_fm(x[b, :T])
        m["mem"] = np.ascontiguousarray(mem[b].reshape(2, 128, D))
        in_maps.append(m)
    res = run_bass_kernel_spmd(nc, in_maps, core_ids=list(range(Bn)), trace=trace)
    out = np.stack([_fm_to_x(np.asarray(r["out"]), T) for r in res.results], axis=0)
    return out, res


def kernel(**inputs):
    out, _ = run(inputs)
    return out.astype(np.float32)
```

```python
import numpy as np
from contextlib import ExitStack
import concourse.bass as bass
import concourse.mybir as mybir
from concourse.bass_utils import run_bass_kernel_spmd

F32 = mybir.dt.float32
BF16 = mybir.dt.bfloat16
AF = mybir.ActivationFunctionType
ALU = mybir.AluOpType

D = 1024
KC = 8
TB = 512
EPS = 1e-6
DK = 128
FFN = 2816
NF = 22
NGA = 68
WSLOT = 2816
NEGBIG = -30000.0

PP_NMIX = 0
PP_GCW = 8
PP_GLUB = 56
PP_CCW = 64
PP_CCB = 188
PP_LNW = 192
PP_LNB = 196
PP_MEMN = 200
PP_GATEB = 208
PP_NFFN = 232
PP_FFW = 240
PP_FFB = 306
PP_GN = 328
PP_NFIN = 329
NPP = 337

C_ID, C_ONE, C_MLE, C_NLS, C_NUI, C_SELL, C_SC0, C_SC1 = range(8)
NCST = 8

import os
INTERLEAVE = os.environ.get('K_INTER', '1') == '1'
PIPE = os.environ.get('K_PIPE', '0') == '1'
A_PER_R = int(os.environ.get('K_APR', '6'))
YMASK = int(os.environ.get('K_YMASK', '1984'))
SPREAD = os.environ.get('K_SPREAD', '1') == '1'
SEM_EPOCH = 16000
N_DMA_SEMS = 24


class Buf:
    __slots__ = ("lw", "rd")

    def __init__(self):
        self.lw = None
        self.rd = {}


class Tl:
    __slots__ = ("ap", "b")

    def __init__(self, ap, b=None):
        self.ap = ap
        self.b = b if b is not None else Buf()

    def __getitem__(self, k):
        return Tl(self.ap[k], self.b)


class KB:
    def __init__(self, nc, stack):
        self.nc = nc
        self.stack = stack
        self.eng = {"pe": nc.tensor, "act": nc.scalar, "dve": nc.vector, "pool": nc.gpsimd, "sp": nc.sync}
        self.sems = {}
        self.cur = {}
        self.epoch = {}
        for e in self.eng:
            self.epoch[e] = 0
            self._new_epoch(e)
        self.seen = {e: {} for e in self.eng}
        self.dsem = []
        for i in range(N_DMA_SEMS):
            k = f"d{i}"
            self.sems[k] = stack.enter_context(nc.semaphore(k))
            self.dsem.append([k, 0])
        self.dnext = 0
        self.qsem = []
        for i in range(8):
            k = f"q{i}"
            self.sems[k] = stack.enter_context(nc.semaphore(k))
            self.qsem.append([k, 0])
        self.qnext = 0
        self.n_instr = 0

    def _new_epoch(self, e):
        k = f"{e}_{self.epoch[e]}"
        self.sems[k] = self.stack.enter_context(self.nc.semaphore(k))
        self.cur[e] = [k, 0]
        self.epoch[e] += 1

    def _wait(self, e, ev):
        k, v = ev
        if self.seen[e].get(k, 0) >= v:
            return
        if e == "pe" and k.startswith("pe_"):
            return
        self.eng[e].wait_ge(self.sems[k], v)
        self.seen[e][k] = v

    @staticmethod
    def _flat(bs):
        o = []
        for b in bs:
            if isinstance(b, (list, tuple)):
                o.extend(b)
            else:
                o.append(b)
        return o

    def _deps(self, e, reads, writes):
        reads = self._flat(reads)
        writes = self._flat(writes)
        for b in reads:
            if b.lw is not None:
                self._wait(e, b.lw)
        for b in writes:
            if b.lw is not None:
                self._wait(e, b.lw)
            for k, v in b.rd.items():
                self._wait(e, (k, v))

    def _mark(self, ev, reads, writes):
        reads = self._flat(reads)
        writes = self._flat(writes)
        for b in writes:
            b.lw = ev
            b.rd = {}
        for b in reads:
            b.rd[ev[0]] = ev[1]

    def op(self, e, fn, reads=(), writes=()):
        self._deps(e, reads, writes)
        ins = fn(self.eng[e])
        c = self.cur[e]
        c[1] += 1
        ins.then_inc(self.sems[c[0]], 1)
        self._mark((c[0], c[1]), reads, writes)
        self.n_instr += 1
        if c[1] >= SEM_EPOCH:
            self._new_epoch(e)

    def dma(self, q, out, in_, reads=(), writes=()):
        if q == "pool":
            d = self.qsem[self.qnext]
            self.qnext = (self.qnext + 1) % len(self.qsem)
        else:
            d = self.dsem[self.dnext]
            self.dnext = (self.dnext + 1) % len(self.dsem)
        if d[1] > 0:
            self._wait(q, (d[0], d[1]))
        self._deps(q, reads, writes)
        ins = self.eng[q].dma_start(out=out, in_=in_)
        d[1] += 16
        ins.then_inc(self.sems[d[0]], 16)
        self._mark((d[0], d[1]), reads, writes)
        self.n_instr += 1

    def finish(self, bufs):
        for b in bufs:
            if b.lw is not None:
                self._wait("sp", b.lw)


def build(T, NL, dbg_names=()):
    NBLK = T // TB
    nc = bass.Bass("TRN2", target_bir_lowering=False)
    x_in = nc.dram_tensor("x", [NBLK, 128, KC * TB], F32, kind="ExternalInput").ap()
    mem_in = nc.dram_tensor("mem", [2, 128, D], F32, kind="ExternalInput").ap()
    cst_in = nc.dram_tensor("cst", [128, NCST * 128], F32, kind="ExternalInput").ap()
    wa_in = [nc.dram_tensor(f"wa{l}", [NGA, 128, 2048], F32, kind="ExternalInput").ap() for l in range(NL)]
    wd_in = [nc.dram_tensor(f"wd{l}", [8, 128, WSLOT], F32, kind="ExternalInput").ap() for l in range(NL)]
    pp_in = [nc.dram_tensor(f"pp{l}", [128, NPP], F32, kind="ExternalInput").ap() for l in range(NL)]
    pb_in = [nc.dram_tensor(f"pb{l}", [128, 8], F32, kind="ExternalInput").ap() for l in range(NL)]
    wab_in = [nc.dram_tensor(f"wab{l}", [128, 64], F32, kind="ExternalInput").ap() for l in range(NL)]
    out_d = nc.dram_tensor("out", [NBLK, 128, KC * TB], F32, kind="ExternalOutput").ap()
    wabf = [nc.dram_tensor(f"wabf{l}", [NGA, 128, 2048], BF16).ap() for l in range(NL)]
    wdbf = [nc.dram_tensor(f"wdbf{l}", [8, 128, WSLOT], BF16).ap() for l in range(NL)]
    xs = [x_in] + [nc.dram_tensor(f"xs{l}", [NBLK, 128, KC * TB], F32).ap() for l in range(1, NL)] + [out_d]
    xs_b = [[Buf() for _ in range(NBLK)] for _ in range(NL + 1)]
    dbg_out = {}

    with ExitStack() as st:
        kb = KB(nc, st)

        def sbt(name, shape, dt):
            return st.enter_context(nc.sbuf_tensor("sb_" + name, shape, dt))

        def rb(*tl):
            return [t.b for t in tl if isinstance(t, Tl)]

        def apof(v):
            return v.ap if isinstance(v, Tl) else v

        def mm(ps, lhsT, rhs, start=True, stop=True):
            kb.op("pe", lambda e: e.matmul(ps.ap, lhsT.ap, rhs.ap, start=start, stop=stop),
                  reads=[lhsT.b, rhs.b], writes=[ps.b])

        def tr(ps, in_, idn):
            kb.op("pe", lambda e: e.transpose(ps.ap, in_.ap, idn.ap), reads=[in_.b, idn.b], writes=[ps.b])

        def act(out, in_, func, bias=None, scale=None, accum=None, eng="act"):
            kw = {}
            if bias is not None:
                kw["bias"] = apof(bias)
            if scale is not None:
                kw["scale"] = apof(scale)
            wr = [out.b]
            if accum is not None:
                kw["accum_out"] = accum.ap
                wr.append(accum.b)
            kb.op("act", lambda e: e.activation(out=out.ap, in_=in_.ap, func=func, **kw),
                  reads=[in_.b] + rb(bias, scale), writes=wr)

        def tt(out, a, b, op, eng="dve"):
            kb.op(eng, lambda e: e.tensor_tensor(out=out.ap, in0=a.ap, in1=b.ap, op=op), reads=[a.b, b.b], writes=[out.b])

        def ts(out, a, s1, op0, s2=None, op1=None, eng="dve"):
            if op1 is None:
                kb.op(eng, lambda e: e.tensor_scalar(out=out.ap, in0=a.ap, scalar1=apof(s1), scalar2=None, op0=op0),
                      reads=[a.b] + rb(s1), writes=[out.b])
            else:
                kb.op(eng, lambda e: e.tensor_scalar(out=out.ap, in0=a.ap, scalar1=apof(s1), scalar2=apof(s2), op0=op0, op1=op1),
                      reads=[a.b] + rb(s1, s2), writes=[out.b])

        def stt(out, in0, scalar, in1, op0, op1):
            kb.op("dve", lambda e: e.scalar_tensor_tensor(out=out.ap, in0=in0.ap, scalar=apof(scalar), in1=in1.ap, op0=op0, op1=op1),
                  reads=[in0.b, in1.b] + rb(scalar), writes=[out.b])

        def rsqrt(out, in_, scale, bias):
            act(out, in_, AF.Ln, bias=bias, scale=scale)
            act(out, out, AF.Exp, scale=-0.5)

        def recip(out, in_):
            kb.op("dve", lambda e: e.reciprocal(out=out.ap, in_=in_.ap), reads=[in_.b], writes=[out.b])

        def cp(out, in_, eng="act"):
            if eng == "act":
                kb.op("act", lambda e: e.copy(out=out.ap, in_=in_.ap), reads=[in_.b], writes=[out.b])
            else:
                kb.op(eng, lambda e: e.tensor_copy(out=out.ap, in_=in_.ap), reads=[in_.b], writes=[out.b])

        def memset(t, v, eng="dve"):
            kb.op(eng, lambda e: e.memset(t.ap, v), writes=[t.b])

        def dbg(name, t, shape, extra=()):
            if name not in dbg_names:
                return
            t = Tl(t.ap, t.b)
            for e_ in extra:
                kb._deps("pool", [e_.b], [])
            d = nc.dram_tensor("dbg_" + name, list(shape), F32, kind="ExternalOutput").ap()
            tmp = Buf()
            kb.dma("pool", d, t.ap, reads=[t.b], writes=[tmp])
            dbg_out[name] = tmp

        cst_t = sbt("cst", [128, NCST * 128], F32)
        cstb = Buf()
        kb.dma("sp", cst_t[:], cst_in, writes=[cstb])
        CST = [Tl(cst_t[:, i * 128:(i + 1) * 128], cstb) for i in range(NCST)]
        ident32, ones32 = CST[C_ID], CST[C_ONE]
        cbf_t = sbt("cbf", [128, 256], BF16)
        ident_bf = Tl(cbf_t[:, 0:128])
        ones_bf = Tl(cbf_t[:, 128:256])
        cp(ident_bf, ident32)
        cp(ones_bf, ones32)

        psum_tiles = [Tl(st.enter_context(nc.psum_tensor(f"ps{i}", [128, 512], F32))[:]) for i in range(8)]
        psn = [0]

        def ps():
            t = psum_tiles[psn[0] % 8]
            psn[0] += 1
            return t

        def psb():
            t = ps()
            return Tl(t.ap.bitcast(BF16), t.b)

        xT_t = sbt("xT", [128, KC, TB], F32)
        xT = [Tl(xT_t[:, c, :]) for c in range(KC)]
        xT_all = Tl(xT_t[:].rearrange("p c t -> p (c t)"))
        hT_t = sbt("hT", [128, KC, TB], BF16)
        hT = [Tl(hT_t[:, c, :]) for c in range(KC)]
        qkv_t = sbt("qkvT", [128, 8, TB], F32)
        qkvT = [Tl(qkv_t[:, j, :]) for j in range(8)]
        qkb_t = sbt("qkb", [128, 12, TB], BF16)
        qkb = [Tl(qkb_t[:, j, :]) for j in range(12)]
        zT_t = sbt("zT", [128, 4, TB], BF16)
        zT = [Tl(zT_t[:, j, :]) for j in range(4)]
        oa_t = sbt("oaT", [128, 4, TB], BF16)
        oaT = [Tl(oa_t[:, j, :]) for j in range(4)]
        uc_t = sbt("ucT", [128, 4, TB], BF16)
        ucT = [Tl(uc_t[:, j, :]) for j in range(4)]
        oc_t = sbt("ocT", [128, 4, TB], BF16)
        ocT = [Tl(oc_t[:, j, :]) for j in range(4)]
        qcT = zT
        af_t = sbt("actf", [128, NF, TB], BF16)
        actf = [Tl(af_t[:, j, :]) for j in range(NF)]
        mgT = actf[0:KC]
        mg_t = af_t
        big_t = sbt("big", [128, 6, TB], F32)
        bigs = [Tl(big_t[:, j, :]) for j in range(6)]
        bign = [0]

        def big():
            t = bigs[bign[0] % 6]
            bign[0] += 1
            return t

        def bigb():
            t = big()
            return Tl(t.ap.bitcast(BF16)[:, 0:TB], t.b)

        pre_t = sbt("pre", [128, 2, 544], BF16)
        pres = [Tl(pre_t[:, j, :]) for j in range(2)]
        pren = [0]
        ub_t = sbt("ubuf", [128, 4, 544], BF16)
        ubuf = [Tl(ub_t[:, j, :]) for j in range(4)]
        hg_t = sbt("halo_g", [128, 12, 4], BF16)
        halo_g = [Tl(hg_t[:, j, :]) for j in range(12)]
        hf_t = sbt("halo_f", [128, NF, 2], BF16)
        halo_f = [Tl(hf_t[:, j, :]) for j in range(NF)]
        dg_t = sbt("diag", [128, 16, 128], BF16)
        dgs = [Tl(dg_t[:, j, :]) for j in range(16)]
        dgn = [0]
        w_t = sbt("wring", [128, 3, WSLOT], BF16)
        wslots = [Tl(w_t[:, j, :]) for j in range(3)]
        S_t = sbt("S", [128, 4, 128], F32)
        S = [Tl(S_t[:, h, :]) for h in range(4)]
        E_t = sbt("E", [128, 2, TB], BF16)
        Ebuf = [Tl(E_t[:, j, :]) for j in range(2)]
        km_t = sbt("kmT", [128, 4, 256], BF16)
        kmT = [Tl(km_t[:, h, :]) for h in range(4)]
        vm_t = sbt("vm", [128, 2, 512], BF16)
        vm = [Tl(vm_t[:, j, :]) for j in range(2)]
        mn_t = sbt("memnT", [128, KC, 256], BF16)
        memnT = [Tl(mn_t[:, c, :]) for c in range(KC)]
        pp_t = [sbt(f"ppT{l}", [128, NPP], F32) for l in range(NL)]
        ppT = [Tl(pp_t[l][:]) for l in range(NL)]
        pb_t = [sbt(f"pbT{l}", [128, 8], F32) for l in range(NL)]
        pbT = [Tl(pb_t[l][:]) for l in range(NL)]
        wab_t = [sbt(f"wabT{l}", [128, KC, 8], BF16) for l in range(NL)]
        wabT = [Tl(wab_t[l][:]) for l in range(NL)]
        negA_t = sbt("negA", [128, 4], F32)
        negA = Tl(negA_t[:])
        gs_names = ["y", "ay", "e1", "l1", "m", "g", "beta", "nbeta", "gcum", "ngcum", "sckd", "egc", "scw", "egl0", "egl1", "tmp"]
        gs_t = sbt("gs", [128, len(gs_names), 16], F32)
        gs = {n: Tl(gs_t[:, i, :]) for i, n in enumerate(gs_names)}
        names_h = ["u", "wT", "qd", "QKT", "kdec", "vnew", "o", "on"]
        gh_t = sbt("gh", [128, len(names_h) * 4, 128], F32)
        GH = {n: [Tl(gh_t[:, i * 4 + h, :]) for h in range(4)] for i, n in enumerate(names_h)}
        gy = [Tl(gh_t[:, 4 * j:4 * j + 4, :].rearrange("p a t -> p (a t)"),
                 [GH[names_h[j]][h].b for h in range(4)]) for j in range(8)]
        names_t = ["diagG", "t1", "Dl", "t2", "Du", "eg"]
        names_tb = ["kbg", "vb", "Pa", "Pb"]
        gtb_t = sbt("gtb", [128, 4 * len(names_tb), 128], BF16)
        names_hb = ["wT", "qd", "QKT", "kdec", "vnew", "on"]
        ghb_t = sbt("ghb", [128, len(names_hb) * 4, 128], BF16)
        sqb_t = sbt("sqb", [128, 8, TB], BF16)
        sqb = [Tl(sqb_t[:, j, :]) for j in range(8)]
        sqd_t = sbt("sqdummy", [128, 128], BF16)
        sqdummy = Tl(sqd_t[:])
        Sb_t = sbt("Sb", [128, 4, 128], BF16)
        Sb = [Tl(Sb_t[:, h, :]) for h in range(4)]
        gt_t = sbt("gt", [128, 4 * len(names_t), 128], F32)
        GT = [{n: Tl(gt_t[:, i * 4 + s, :]) for i, n in enumerate(names_t)} for s in range(4)]
        for s_ in range(4):
            for i, n in enumerate(names_tb):
                GT[s_][n] = Tl(gtb_t[:, i * 4 + s_, :])

        def wide(tensor, idx0, tls):
            return Tl(tensor[:, idx0:idx0 + 4, :], [t_.b for t_ in tls])
        W4 = {n: wide(gt_t, i * 4, [GT[h_][n] for h_ in range(4)]) for i, n in enumerate(names_t)}
        for i, n in enumerate(names_tb):
            W4[n] = wide(gtb_t, i * 4, [GT[h_][n] for h_ in range(4)])
        _ghb0 = {n: [Tl(ghb_t[:, i * 4 + h, :]) for h in range(4)] for i, n in enumerate(names_hb)}
        GHbS = [_ghb0, _ghb0]
        GuS = [GH["u"], GH["u"]]
        qx_t = sbt("qx", [128, 2, 4, 256], BF16)
        QX = [[Tl(qx_t[:, i, s, :]) for i in range(2)] for s in range(4)]
        QX4 = [Tl(qx_t[:, i, :, :], [QX[h_][i].b for h_ in range(4)]) for i in range(2)]
        on4_t = sbt("ones4", [128, 4, 128], F32)
        ones4 = [Tl(on4_t[:, h, :]) for h in range(4)]
        for h in range(4):
            memset(ones4[h], 1.0)
        ss_t = sbt("ss", [128, 16], F32)
        ss = Tl(ss_t[:, 0:4]); ssd = Tl(ss_t[:, 4:8]); srs = Tl(ss_t[:, 8:12])
        mss = Tl(ss_t[:, 12:13]); msd = Tl(ss_t[:, 13:14]); mrs = Tl(ss_t[:, 14:15])

        wa_b = [[Buf() for _ in range(NGA)] for _ in range(NL)]
        wd_b = [[Buf() for _ in range(8)] for _ in range(NL)]
        blk_seq = [("a", g) for g in range(14)]
        for m in range(4):
            blk_seq += [("a", 26 + m), ("a", 30 + m), ("a", 34 + m), ("a", 14 + m), ("a", 18 + m), ("a", 22 + m)]
        blk_seq += [("a", 38 + m) for m in range(4)]
        for fg in range(11):
            blk_seq += [("a", 42 + fg), ("a", 53 + fg)]
        blk_seq += [("d", o) for o in range(8)]
        lay_seq = [("a", 64), ("a", 65), ("a", 66), ("a", 67)]
        allseq = []
        for l in range(NL):
            allseq += [(l,) + s for s in lay_seq]
            for b in range(NBLK):
                allseq += [(l,) + s for s in blk_seq]
        done = set()
        for (l, kind, g) in allseq:
            if (l, kind, g) in done:
                continue
            done.add((l, kind, g))
            if kind == "a":
                kb.dma("pool", wabf[l][g], wa_in[l][g], writes=[wa_b[l][g]])
            else:
                kb.dma("pool", wdbf[l][g], wd_in[l][g], writes=[wd_b[l][g]])

        wstate = {"issued": 0, "next": 0}
        LOOK = 2

        def nextw(l, kind, g):
            i = wstate["next"]
            assert allseq[i] == (l, kind, g), (allseq[i], (l, kind, g))
            while wstate["issued"] < min(i + 1 + LOOK, len(allseq)):
                j = wstate["issued"]
                (l2, k2, g2) = allseq[j]
                slot = wslots[j % 3]
                if k2 == "a":
                    kb.dma("sp", slot.ap[:, 0:2048], wabf[l2][g2], reads=[wa_b[l2][g2]], writes=[slot.b])
                else:
                    kb.dma("sp", slot.ap, wdbf[l2][g2], reads=[wd_b[l2][g2]], writes=[slot.b])
                wstate["issued"] += 1
            wstate["next"] += 1
            slot = wslots[i % 3]
            if kind == "a":
                return Tl(slot.ap[:, 0:2048].rearrange("p (k n) -> p k n", k=KC), slot.b)
            return Tl(slot.ap.rearrange("p (f n) -> p f n", f=NF), slot.b)

        def ppc(l, col, n=1):
            return Tl(pp_t[l][:, col:col + n], ppT[l].b)

        def diag(l, col):
            d = dgs[dgn[0] % 16]
            dgn[0] += 1
            if dgn[0] % 2 == 0:
                kb.op("dve", lambda e: e.tensor_scalar(out=d.ap, in0=ident_bf.ap, scalar1=pp_t[l][:, col:col + 1], scalar2=None, op0=ALU.mult),
                      reads=[ident_bf.b, ppT[l].b], writes=[d.b])
            else:
                kb.op("act", lambda e: e.activation(out=d.ap, in_=ident_bf.ap, func=AF.Identity, scale=pp_t[l][:, col:col + 1]),
                      reads=[ident_bf.b, ppT[l].b], writes=[d.b])
            return d

        def proj(W, cols, src, nk):
            p = ps()
            for kc in range(nk):
                mm(p, W[:, kc, cols], src[kc], start=(kc == 0), stop=(kc == nk - 1))
            return p

        def rmsnorm(l, wcol, outs):
            pss = ps()
            for c in range(KC):
                sq = bigb()
                act(sq, xT[c], AF.Square)
                mm(pss, ones_bf, sq, start=(c == 0), stop=(c == KC - 1))
            rstd = big()
            rsqrt(rstd, pss, 1.0 / D, EPS)
            for c in range(KC):
                stt(outs[c], xT[c], ppc(l, wcol + c), rstd, ALU.mult, ALU.mult)

        for l in range(NL):
            kb.dma("sp", pp_t[l][:], pp_in[l], writes=[ppT[l].b])
            kb.dma("sp", pb_t[l][:], pb_in[l], writes=[pbT[l].b])
            kb.dma("pool", wab_t[l][:].rearrange("p k n -> p (k n)"), wab_in[l], writes=[wabT[l].b])
            act(negA, Tl(pb_t[l][:, 4:8], pbT[l].b), AF.Exp)
            ts(negA, negA, -1.0, ALU.mult)
            for h in range(4):
                memset(S[h], 0.0)
                memset(Sb[h], 0.0)
            for j in range(12):
                memset(halo_g[j], 0.0, eng="pool")
            for f in range(NF):
                memset(halo_f[f], 0.0, eng="pool")
            for c in range(4):
                memset(ubuf[c][:, 0:32], 0.0, eng="pool")
            memt = [Tl(qkv_t[:, 2 * mt:2 * mt + 2, :].rearrange("p a t -> p (a t)"), qkvT[2 * mt].b) for mt in range(2)]
            for mt in range(2):
                kb.dma("sp", memt[mt].ap, mem_in[mt], writes=[memt[mt].b, qkvT[2 * mt + 1].b])
            for mt in range(2):
                sq = big()
                for hh in range(2):
                    act(sq, memt[mt][:, hh * 512:(hh + 1) * 512], AF.Square, accum=Tl(ss_t[:, 12 + hh:13 + hh], mss.b))
                tt(mss, Tl(ss_t[:, 12:13], mss.b), Tl(ss_t[:, 13:14], mss.b), ALU.add)
                act(msd, mss, AF.Sqrt, bias=EPS, scale=1.0 / D)
                recip(mrs, msd)
                for hh in range(2):
                    act(memt[mt][:, hh * 512:(hh + 1) * 512], memt[mt][:, hh * 512:(hh + 1) * 512], AF.Identity, scale=mrs)
            for c in range(KC):
                p = ps()
                for mt in range(2):
                    tr(p[:, mt * 128:(mt + 1) * 128], memt[mt][:, c * 128:(c + 1) * 128], ident32)
                ts(memnT[c], p[:, 0:256], ppc(l, PP_MEMN + c), ALU.mult)
            for g in range(2):
                W = nextw(l, "a", 64 + g)
                for hh in range(2):
                    h = 2 * g + hh
                    p = ps()
                    for kc in range(KC):
                        mm(p[:, 0:256], W[:, kc, hh * 128:(hh + 1) * 128], memnT[kc], start=(kc == 0), stop=(kc == KC - 1))
                    cp(kmT[h], p[:, 0:256])
            for g in range(2):
                W = nextw(l, "a", 66 + g)
                for mt in range(2):
                    p = ps()
                    for kc in range(KC):
                        mm(p[:, 0:256], memnT[kc][:, mt * 128:(mt + 1) * 128], W[:, kc, :], start=(kc == 0), stop=(kc == KC - 1))
                    cp(vm[mt][:, g * 256:(g + 1) * 256], p[:, 0:256])

            for b in range(NBLK):
                kb.dma("sp", xT_t[:].rearrange("p c t -> p (c t)"), xs[l][b], reads=[xs_b[l][b]], writes=[t.b for t in xT])
                rmsnorm(l, PP_NMIX, hT)

                pab = ps()
                for t in range(4):
                    for kc in range(KC):
                        mm(pab[:, t * 8:(t + 1) * 8], hT[kc][:, t * 128:(t + 1) * 128], wabT[l][:, kc, :], start=(kc == 0), stop=(kc == KC - 1))
                pab3 = Tl(pab.ap[:, 0:32].rearrange("p (t e) -> p t e", t=4), pab.b)
                v3 = lambda t_: Tl(t_.ap.rearrange("p (t e) -> p t e", t=4), t_.b)
                dtb = Tl(pb_t[l][:, 0:4].unsqueeze(1).to_broadcast([128, 4, 4]), pbT[l].b)
                nA3 = Tl(negA_t[:, 0:4].unsqueeze(1).to_broadcast([128, 4, 4]), negA.b)
                tt(v3(gs["y"]), pab3[:, :, 0:4], dtb, ALU.add)
                act(v3(gs["beta"]), pab3[:, :, 4:8], AF.Sigmoid)
                pch = {}

                def c_gcum():
                    pch["g"] = ps()
                    mm(pch["g"][:, 0:16], CST[C_MLE], gs["g"])
                    cp(gs["gcum"], pch["g"][:, 0:16])

                def c_sell():
                    pch["l"] = ps()
                    mm(pch["l"][:, 0:16], CST[C_SELL], gs["gcum"])
                    tt(gs["tmp"], pch["l"][:, 0:16], gs["gcum"], ALU.subtract)

                def c_egl(c2):
                    p_ = ps()
                    mm(p_[:, 0:16], CST[C_SC0 + c2], gs["gcum"])
                    act(gs["egl%d" % c2], p_[:, 0:16], AF.Exp)

                chain = [
                    lambda: ts(gs["tmp"], gs["y"], -1.0, ALU.mult),
                    lambda: tt(gs["ay"], gs["y"], gs["tmp"], ALU.max),
                    lambda: act(gs["e1"], gs["ay"], AF.Exp, scale=-1.0),
                    lambda: act(gs["l1"], gs["e1"], AF.Ln, bias=1.0),
                    lambda: ts(gs["m"], gs["y"], 0.0, ALU.max),
                    lambda: tt(gs["m"], gs["m"], gs["l1"], ALU.add),
                    lambda: tt(v3(gs["g"]), v3(gs["m"]), nA3, ALU.mult),
                    lambda: ts(gs["nbeta"], gs["beta"], -1.0, ALU.mult),
                    c_gcum,
                    lambda: ts(gs["ngcum"], gs["gcum"], -1.0, ALU.mult),
                    c_sell,
                    lambda: act(gs["sckd"], gs["tmp"], AF.Exp),
                    lambda: act(gs["egc"], gs["gcum"], AF.Exp),
                    lambda: tt(gs["scw"], gs["beta"], gs["egc"], ALU.mult),
                    lambda: c_egl(0),
                    lambda: c_egl(1),
                ]

                W = None
                sqs = {}
                for j in range(12):
                    if j % 2 == 0:
                        W = nextw(l, "a", j // 2)
                    p = proj(W, slice((j % 2) * 128, (j % 2) * 128 + 128), hT, KC)
                    pre = pres[pren[0] % 2]
                    pren[0] += 1
                    cp(pre[:, 0:3], halo_g[j][:, 0:3], eng="pool")
                    cp(pre[:, 3:515], p, eng="dve")
                    cp(halo_g[j][:, 0:3], pre[:, 512:515], eng="pool")
                    acc = big()
                    ts(acc, pre[:, 0:TB], ppc(l, PP_GCW + j * 4), ALU.mult)
                    for k in range(1, 4):
                        stt(acc, pre[:, k:k + TB], ppc(l, PP_GCW + j * 4 + k), acc, ALU.mult, ALU.add)
                    act(qkvT[j] if j < 8 else qkb[j], acc, AF.Silu)
                    if j < 8:
                        act(sqb[j], qkvT[j], AF.Square)
                    for _ in range(2):
                        if chain:
                            chain.pop(0)()
                while chain:
                    chain.pop(0)()
                for j in range(4):
                    if j % 2 == 0:
                        W = nextw(l, "a", 6 + j // 2)
                    p = proj(W, slice((j % 2) * 128, (j % 2) * 128 + 128), hT, KC)
                    act(zT[j], p, AF.Silu)

                def l2norm(j):
                    p_ = ps()
                    mm(p_, ones_bf, sqb[j])
                    rn = big()
                    rsqrt(rn, p_, 1.0, EPS)
                    stt(qkb[j], qkvT[j], (DK ** -0.5) if j < 4 else 1.0, rn, ALU.mult, ALU.mult)

                for c in range(4):
                    if c % 2 == 0:
                        W = nextw(l, "a", 8 + c // 2)
                    p = proj(W, slice((c % 2) * 128, (c % 2) * 128 + 128), hT, KC)
                    cp(gy[c], p)
                    l2norm(c)
                for c in range(4):
                    if c % 2 == 0:
                        W = nextw(l, "a", 10 + c // 2)
                    p = proj(W, slice((c % 2) * 128, (c % 2) * 128 + 128), hT, KC)
                    sg = big()
                    act(sg, p, AF.Sigmoid, bias=ppc(l, PP_GLUB + 4 + c))
                    stt(ubuf[c][:, 30:542], gy[c], ppc(l, PP_GLUB + c), sg, ALU.add, ALU.mult)
                    l2norm(4 + c)
                def col(name, t, h):
                    return Tl(gs[name].ap[:, t * 4 + h:t * 4 + h + 1], gs[name].b)

                def bc_mid(ap2d):
                    return ap2d.unsqueeze(1).to_broadcast([128, 4, 128])

                def bc_last(ap2d):
                    return ap2d.unsqueeze(2).to_broadcast([128, 4, 128])

                def p3(pt):
                    return Tl(pt.ap[:, 0:512].rearrange("p (h f) -> p h f", h=4), pt.b)

                def pre4(t):
                    tsl = slice(t * 128, (t + 1) * 128)
                    GHb = GHbS[t % 2]
                    gsl = lambda n: Tl(bc_last(gs[n].ap[:, t * 4:(t + 1) * 4]), gs[n].b)
                    gcum_b, nbeta_b, scw_b, sckd_b, beta_b = gsl("gcum"), gsl("nbeta"), gsl("scw"), gsl("sckd"), gsl("beta")
                    idb32 = Tl(bc_mid(ident32.ap), ident32.b)
                    idbbf = Tl(bc_mid(ident_bf.ap), ident_bf.b)
                    nls_b = Tl(bc_mid(CST[C_NLS].ap), CST[C_NLS].b)
                    nui_b = Tl(bc_mid(CST[C_NUI].ap), CST[C_NUI].b)
                    flat = lambda w_: Tl(w_.ap.rearrange("p h f -> p (h f)"), w_.b)
                    qd4 = Tl(ghb_t[:, (names_hb.index("qd")) * 4:][:, 0:4, :], [GHb["qd"][h_].b for h_ in range(4)])
                    qkt4 = Tl(ghb_t[:, (names_hb.index("QKT")) * 4:][:, 0:4, :], [GHb["QKT"][h_].b for h_ in range(4)])
                    kdec4 = Tl(ghb_t[:, (names_hb.index("kdec")) * 4:][:, 0:4, :], [GHb["kdec"][h_].b for h_ in range(4)])
                    q4 = Tl(qkb_t[:, 0:4, tsl], [qkb[h_].b for h_ in range(4)])
                    tt(W4["diagG"], idb32, gcum_b, ALU.mult)
                    pG = ps()
                    mm(pG, ones32, flat(W4["diagG"]))
                    stt(W4["t1"], p3(pG), -1.0, gcum_b, ALU.mult, ALU.add)
                    tt(W4["t2"], nui_b, W4["t1"], ALU.subtract)
                    tt(W4["t1"], W4["t1"], nls_b, ALU.add)
                    act(flat(W4["Dl"]), flat(W4["t1"]), AF.Exp)
                    act(flat(W4["Du"]), flat(W4["t2"]), AF.Exp)
                    act(flat(W4["eg"]), pG, AF.Exp)
                    tt(qd4, q4, W4["eg"], ALU.mult)
                    pK = ps()
                    for h in range(4):
                        kTn = qkb[4 + h][:, tsl]
                        mm(pK[:, h * 128:(h + 1) * 128], kTn, kTn)
                    tt(W4["t2"], p3(pK), W4["Dl"], ALU.mult)
                    tt(W4["Pa"], W4["t2"], nbeta_b, ALU.mult)
                    pT = psb()
                    for h in range(4):
                        tr(pT[:, h * 128:(h + 1) * 128], GT[h]["Pa"], ident_bf)
                    pT3 = Tl(pT.ap[:, 0:512].rearrange("p (h f) -> p h f", h=4), pT.b)
                    cp(Tl(QX4[0].ap[:, :, 0:128], QX4[0].b), pT3)
                    tt(Tl(QX4[0].ap[:, :, 128:256], QX4[0].b), pT3, idbbf, ALU.add)
                    pQ = ps()
                    for h in range(4):
                        mm(pQ[:, h * 128:(h + 1) * 128], qkb[4 + h][:, tsl], qkb[h][:, tsl])
                    tt(qkt4, p3(pQ), W4["Du"], ALU.mult)
                    pk = psb()
                    for h in range(4):
                        tr(pk[:, h * 128:(h + 1) * 128], qkb[4 + h][:, tsl], ident_bf)
                    pk3 = Tl(pk.ap[:, 0:512].rearrange("p (h f) -> p h f", h=4), pk.b)
                    tt(W4["kbg"], pk3, scw_b, ALU.mult)
                    tt(kdec4, pk3, sckd_b, ALU.mult)
                    pv = psb()
                    for h in range(4):
                        tr(pv[:, h * 128:(h + 1) * 128], qkb[8 + h][:, tsl], ident_bf)
                    pv3 = Tl(pv.ap[:, 0:512].rearrange("p (h f) -> p h f", h=4), pv.b)
                    tt(W4["vb"], pv3, beta_b, ALU.mult)
                    pP = ps()
                    pQ1 = ps()
                    for h in range(4):
                        mm(pP[:, h * 128:(h + 1) * 128], QX[h][0][:, 0:128], GT[h]["Pa"])
                        mm(pQ1[:, h * 128:(h + 1) * 128], GT[h]["Pa"], QX[h][0][:, 0:128])
                    cp(W4["Pb"], p3(pP))
                    cp(Tl(QX4[1].ap[:, :, 0:128], QX4[1].b), p3(pQ1))
                    cp(Tl(QX4[1].ap[:, :, 128:256], QX4[1].b), Tl(QX4[0].ap[:, :, 128:256], QX4[0].b), eng="dve")

                def head_gen(h, t, tsl):
                    GHb = GHbS[t % 2]
                    Gu = GuS[t % 2]
                    G = GT[h]
                    Pk, Pn = G["Pb"], G["Pa"]
                    qk_, qn_ = QX[h][1], QX[h][0]
                    for lev in range(1, 6):
                        pX = ps()
                        if lev < 5:
                            mm(pX[:, 0:256], Pk, qk_[:, 0:256])
                            pP = ps()
                            mm(pP[:, 0:128], qk_[:, 0:128], Pk)
                            cp(qn_[:, 0:128], pX[:, 0:128])
                            tt(qn_[:, 128:256], qk_[:, 128:256], pX[:, 128:256], ALU.add)
                            cp(Pn, pP[:, 0:128])
                            Pk, Pn = Pn, Pk
                            qk_, qn_ = qn_, qk_
                            yield "neu"
                        else:
                            mm(pX[:, 0:128], Pk, qk_[:, 128:256])
                            tt(qn_[:, 128:256], qk_[:, 128:256], pX[:, 0:128], ALU.add)
                            qk_, qn_ = qn_, qk_
                            yield "neu"
                    TT = qk_[:, 128:256]
                    pu = ps()
                    mm(pu[:, 0:128], TT, G["vb"])
                    cp(Gu[h], pu[:, 0:128])
                    yield "neu"
                    pw = ps()
                    mm(pw[:, 0:128], G["kbg"], TT)
                    cp(GHb["wT"][h], pw[:, 0:128], eng="dve")
                    yield "neu"

                def A_gen(t):
                    tsl = slice(t * 128, (t + 1) * 128)
                    pre4(t)
                    yield
                    gens = [head_gen(h, t, tsl) for h in range(4)]
                    while gens:
                        for g_ in list(gens):
                            try:
                                next(g_)
                                yield
                            except StopIteration:
                                gens.remove(g_)

                def R_gen(t):
                    tsl = slice(t * 128, (t + 1) * 128)
                    GHb = GHbS[t % 2]
                    Gu = GuS[t % 2]
                    hb = lambda n: [GHb[n][h_].b for h_ in range(4)]
                    vnew4 = Tl(ghb_t[:, names_hb.index("vnew") * 4:][:, 0:4, :], hb("vnew"))
                    on4 = Tl(ghb_t[:, names_hb.index("on") * 4:][:, 0:4, :], hb("on"))
                    u4 = Tl(gh_t[:, names_h.index("u") * 4:][:, 0:4, :], [Gu[h_].b for h_ in range(4)])
                    o4 = Tl(gh_t[:, names_h.index("o") * 4:][:, 0:4, :], [GH["o"][h_].b for h_ in range(4)])
                    osq4 = Tl(gh_t[:, names_h.index("on") * 4:][:, 0:4, :], [GH["on"][h_].b for h_ in range(4)])
                    S4 = Tl(S_t[:], [S[h_].b for h_ in range(4)])
                    Sb4 = Tl(Sb_t[:], [Sb[h_].b for h_ in range(4)])
                    for c2 in range(2):
                        r = slice(c2 * 64, (c2 + 1) * 64)
                        egl_b = Tl(bc_last(gs["egl%d" % c2].ap[:, t * 4:(t + 1) * 4]), gs["egl%d" % c2].b)
                        p1 = ps()
                        for h in range(4):
                            mm(p1[:, h * 128:(h + 1) * 128], GHb["wT"][h], Sb[h])
                        tt(vnew4[r], u4[r], p3(p1)[r], ALU.subtract)
                        tt(S4, S4, egl_b, ALU.mult)
                        yield
                        p2 = ps()
                        p3_ = ps()
                        for h in range(4):
                            mm(p2[:, h * 128:(h + 1) * 128], GHb["qd"][h], Sb[h], start=True, stop=False)
                            mm(p2[:, h * 128:(h + 1) * 128], GHb["QKT"][h][r, :], GHb["vnew"][h][r, :], start=False, stop=True)
                            mm(p3_[:, h * 128:(h + 1) * 128], GHb["kdec"][h][r, :], GHb["vnew"][h][r, :])
                        tt(Sb4, S4, p3(p3_), ALU.add)
                        tt(S4, S4, p3(p3_), ALU.add)
                        cp(o4[r], p3(p2)[r])
                        yield
                    yield "norm"
                    tt(osq4, o4, o4, ALU.mult)
                    kb.op("dve", lambda e: e.tensor_reduce(out=ss.ap, in_=osq4.ap, axis=mybir.AxisListType.X, op=ALU.add),
                          reads=osq4.b, writes=[ss.b])
                    act(ssd, ss, AF.Sqrt, bias=EPS, scale=1.0 / 128)
                    recip(srs, ssd)
                    tt(on4, o4, Tl(bc_last(ss_t[:, 8:12]), srs.b), ALU.mult)
                    yield
                    po = psb()
                    for h in range(4):
                        tr(po[:, h * 128:(h + 1) * 128], GHb["on"][h], ident_bf)
                    po3 = Tl(po.ap[:, 0:512].rearrange("p (h f) -> p h f", h=4), po.b)
                    stt(Tl(oa_t[:, 0:4, tsl], [oaT[h_].b for h_ in range(4)]), po3, ppc(l, PP_GN),
                        Tl(zT_t[:, 0:4, tsl], [zT[h_].b for h_ in range(4)]), ALU.mult, ALU.mult)
                    yield

                for _ in A_gen(0):
                    pass
                for t in range(4):
                    R = R_gen(t)
                    for tag in R:
                        if tag == "norm":
                            break
                    A = A_gen(t + 1) if t < 3 else None
                    if A is not None:
                        next(A)
                    for _ in R:
                        pass
                    if A is not None:
                        for _ in A:
                            pass

                if l == 0 and b == 0:
                    dbg("hT", Tl(hT_t[:].rearrange("p c t -> p (c t)"), hT[0].b), [128, KC * TB])
                    dbg("qkvT", Tl(qkv_t[:].rearrange("p c t -> p (c t)"), qkvT[0].b), [128, 12 * TB])
                    dbg("oaT", Tl(oa_t[:].rearrange("p c t -> p (c t)"), oaT[0].b), [128, 4 * TB])
                    dbg("gs", Tl(gs_t[:].rearrange("p c t -> p (c t)"), gs["g"].b), [128, len(gs_names) * 16])
                for c in range(4):
                    pc = ps()
                    for k in range(31):
                        dgk = diag(l, PP_CCW + c * 31 + k)
                        mm(pc, dgk, ubuf[c][:, k:k + TB], start=(k == 0), stop=(k == 30))
                    act(gy[c], pc, AF.Identity, bias=ppc(l, PP_CCB + c))
                    cp(ubuf[c][:, 0:30], ubuf[c][:, 512:542], eng="pool")
                pm = ps()
                for c in range(4):
                    mm(pm, ones32, gy[c], start=(c == 0), stop=(c == 3))
                pq = ps()
                for c in range(4):
                    sq = bigb()
                    act(sq, gy[c], AF.Square)
                    mm(pq, ones_bf, sq, start=(c == 0), stop=(c == 3))
                mean = gy[4]
                act(mean, pm, AF.Identity, scale=1.0 / 512)
                msq = gy[5]
                tt(msq, mean, mean, ALU.mult)
                var = gy[6]
                stt(var, pq, 1.0 / 512, msq, ALU.mult, ALU.subtract)
                rstd = gy[7]
                rsqrt(rstd, var, 1.0, EPS)
                for c in range(4):
                    dlt = big()
                    tt(dlt, gy[c], mean, ALU.subtract)
                    tt(dlt, dlt, rstd, ALU.mult)
                    act(ucT[c], dlt, AF.Silu, bias=ppc(l, PP_LNB + c), scale=ppc(l, PP_LNW + c))

                for h in range(4):
                    if h % 2 == 0:
                        W = nextw(l, "a", 12 + h // 2)
                    p = proj(W, slice((h % 2) * 128, (h % 2) * 128 + 128), hT, KC)
                    cp(qcT[h], p)
                for h in range(4):
                    for mt in range(2):
                        psc = ps()
                        mm(psc, kmT[h][:, mt * 128:(mt + 1) * 128], qcT[h])
                        act(Ebuf[mt], psc, AF.Exp, scale=DK ** -0.5)
                    po = ps()
                    pd = ps()
                    for mt in range(2):
                        mm(po, vm[mt][:, h * 128:(h + 1) * 128], Ebuf[mt], start=(mt == 0), stop=(mt == 1))
                        mm(pd, ones_bf, Ebuf[mt], start=(mt == 0), stop=(mt == 1))
                    rden = big()
                    act(rden, pd, AF.Ln)
                    act(rden, rden, AF.Exp, scale=-1.0)
                    tt(ocT[h], po, rden, ALU.mult)

                if l == 0 and b == 0:
                    dbg("ucT", Tl(uc_t[:].rearrange("p c t -> p (c t)"), ucT[0].b), [128, 4 * TB], ucT)
                    dbg("ocT", Tl(oc_t[:].rearrange("p c t -> p (c t)"), ocT[0].b), [128, 4 * TB], ocT)
                for m in range(4):
                    srcs = [oaT, ucT, ocT]
                    for i in range(3):
                        W = nextw(l, "a", 26 + 4 * i + m)
                        Wv = Tl(W.ap.rearrange("p k n -> p (k n)")[:, 0:1024].rearrange("p (k n) -> p k n", k=4), W.b)
                        for jj in range(2):
                            p = proj(Wv, slice(jj * 128, jj * 128 + 128), srcs[i], 4)
                            cp(gy[i * 2 + jj], p)
                    for i in range(3):
                        W = nextw(l, "a", 14 + 4 * i + m)
                        for jj in range(2):
                            j = 2 * m + jj
                            p = proj(W, slice(jj * 128, jj * 128 + 128), hT, KC)
                            gt_ = big()
                            act(gt_, p, AF.Sigmoid, bias=ppc(l, PP_GATEB + i * 8 + j))
                            if i == 0:
                                tt(gy[6 + jj], gy[jj], gt_, ALU.mult)
                            elif i == 1:
                                tt(gt_, gy[2 + jj], gt_, ALU.mult)
                                tt(gy[6 + jj], gy[6 + jj], gt_, ALU.add)
                            else:
                                tt(gt_, gy[4 + jj], gt_, ALU.mult)
                                tt(mgT[j], gy[6 + jj], gt_, ALU.add)
                for m in range(4):
                    W = nextw(l, "a", 38 + m)
                    for jj in range(2):
                        o = 2 * m + jj
                        p = proj(W, slice(jj * 128, jj * 128 + 128), mgT, KC)
                        tt(xT[o], xT[o], p, ALU.add)

                if l == 0 and b == 0:
                    dbg("mgT", Tl(mg_t[:, 0:KC, :].rearrange("p c t -> p (c t)"), mgT[0].b), [128, KC * TB], mgT)
                    dbg("xmid", Tl(xT_t[:].rearrange("p c t -> p (c t)"), xT[0].b), [128, KC * TB], xT)
                rmsnorm(l, PP_NFFN, hT)
                for fg in range(11):
                    Wg = nextw(l, "a", 42 + fg)
                    sgs = []
                    for jj in range(2):
                        f = 2 * fg + jj
                        cols = slice(jj * 128, jj * 128 + 128)
                        p = proj(Wg, cols, hT, KC)
                        pre = pres[pren[0] % 2]
                        pren[0] += 1
                        cp(pre[:, 0:2], halo_f[f][:, 0:2], eng="pool")
                        cp(pre[:, 2:514], p)
                        cp(halo_f[f][:, 0:2], pre[:, 512:514], eng="pool")
                        sg = big()
                        ts(sg, pre[:, 0:TB], ppc(l, PP_FFW + f * 3), ALU.mult)
                        for k in range(1, 3):
                            stt(sg, pre[:, k:k + TB], ppc(l, PP_FFW + f * 3 + k), sg, ALU.mult, ALU.add)
                        act(sg, sg, AF.Silu, bias=ppc(l, PP_FFB + f))
                        sgs.append(sg)
                    Wv = nextw(l, "a", 53 + fg)
                    for jj in range(2):
                        f = 2 * fg + jj
                        cols = slice(jj * 128, jj * 128 + 128)
                        pv = proj(Wv, cols, hT, KC)
                        tt(actf[f], pv, sgs[jj], ALU.mult)
                if l == 0 and b == 0:
                    dbg("hT2", Tl(hT_t[:].rearrange("p c t -> p (c t)"), hT[0].b), [128, KC * TB], hT)
                    dbg("actf", Tl(af_t[:].rearrange("p c t -> p (c t)"), actf[0].b), [128, NF * TB], actf)
                for o in range(8):
                    Wd = nextw(l, "d", o)
                    p = ps()
                    for f in range(NF):
                        mm(p, Wd[:, f, :], actf[f], start=(f == 0), stop=(f == NF - 1))
                    tt(xT[o], xT[o], p, ALU.add)

                if l == 0 and b == 0:
                    dbg("xend", Tl(xT_t[:].rearrange("p c t -> p (c t)"), xT[0].b), [128, KC * TB], xT)
                if l == NL - 1:
                    rmsnorm(l, PP_NFIN, qkvT[0:8])
                    kb.dma("sp", xs[l + 1][b], qkv_t[:, 0:8, :].rearrange("p c t -> p (c t)"),
                           reads=[t.b for t in qkvT[0:8]], writes=[xs_b[l + 1][b]])
                else:
                    kb.dma("sp", xs[l + 1][b], xT_t[:].rearrange("p c t -> p (c t)"),
                           reads=[t.b for t in xT], writes=[xs_b[l + 1][b]])

        kb.finish(xs_b[NL] + list(dbg_out.values()))
        print("instructions:", kb.n_instr, "sems:", len(kb.sems))
    return nc


def _consts():
    idx = np.arange(128)
    ch = idx // 64
    same = ch[:, None] == ch[None, :]
    c = np.zeros((NCST, 128, 128), np.float32)
    c[C_ID] = np.eye(128)
    c[C_ONE] = 1.0
    c[C_MLE] = (same & (idx[:, None] <= idx[None, :]))
    c[C_NLS] = np.where(same & (idx[:, None] > idx[None, :]), 0.0, NEGBIG)
    c[C_NUI] = np.where(same & (idx[:, None] <= idx[None, :]), 0.0, NEGBIG)
    c[C_SELL] = (idx[:, None] == (ch[None, :] * 64 + 63))
    c[C_SC0] = (idx[:, None] == 63) & np.ones((1, 128), bool)
    c[C_SC1] = (idx[:, None] == 127) & np.ones((1, 128), bool)
    return np.ascontiguousarray(c.transpose(1, 0, 2).reshape(128, NCST * 128))


def _grp(w, ncols=256):
    K, N = w.shape
    kc = K // 128
    g = w.reshape(kc, 128, N // ncols, ncols).transpose(2, 1, 0, 3).reshape(N // ncols, 128, kc * ncols)
    return g


def _fm(v, n):
    return np.asarray(v, np.float32).reshape(n, 128).T


def _prep_layer(inp, l):
    w_in = np.asarray(inp["w_in"][l], np.float32)
    cols = np.r_[0:1536, 1544:2056, 2056:3080, 3080:3592, 3592:6664]
    wmain = w_in[:, cols]
    groups = [_grp(wmain)]
    for nm in ("w_gdn_out", "w_cc_out", "w_xa_out"):
        g = _grp(np.asarray(inp[nm][l], np.float32))
        groups.append(np.concatenate([g, np.zeros_like(g)], axis=2))
    groups.append(_grp(np.asarray(inp["w_o"][l], np.float32)))
    groups.append(_grp(np.asarray(inp["w_up"][l], np.float32)))
    groups.append(_grp(np.asarray(inp["w_mem_kv"][l], np.float32)))
    wa = np.ascontiguousarray(np.concatenate(groups, axis=0))
    assert wa.shape == (NGA, 128, 2048), wa.shape
    wd = np.ascontiguousarray(_grp(np.asarray(inp["w_down"][l], np.float32), 128))
    wab = np.ascontiguousarray(w_in[:, 1536:1544].reshape(8, 128, 8).transpose(1, 0, 2).reshape(128, 64))
    pp = np.zeros((128, NPP), np.float32)
    pp[:, PP_NMIX:PP_NMIX + 8] = _fm(inp["norm_mix"][l], 8)
    gcw = np.asarray(inp["gdn_conv_w"][l], np.float32)
    pp[:, PP_GCW:PP_GCW + 48] = gcw.reshape(4, 12, 128).transpose(2, 1, 0).reshape(128, 48)
    pp[:, PP_GLUB:PP_GLUB + 8] = _fm(inp["cc_glu_b"][l], 8)
    ccw = np.asarray(inp["cc_dw_w"][l], np.float32)
    pp[:, PP_CCW:PP_CCW + 124] = ccw.reshape(31, 4, 128).transpose(2, 1, 0).reshape(128, 124)
    pp[:, PP_CCB:PP_CCB + 4] = _fm(inp["cc_dw_b"][l], 4)
    pp[:, PP_LNW:PP_LNW + 4] = _fm(inp["cc_ln_w"][l], 4)
    pp[:, PP_LNB:PP_LNB + 4] = _fm(inp["cc_ln_b"][l], 4)
    pp[:, PP_MEMN:PP_MEMN + 8] = _fm(inp["mem_norm"][l], 8)
    pp[:, PP_GATEB:PP_GATEB + 24] = _fm(inp["gate_b"][l], 24)
    pp[:, PP_NFFN:PP_NFFN + 8] = _fm(inp["norm_ffn"][l], 8)
    ffw = np.asarray(inp["ffn_dw_w"][l], np.float32)
    pp[:, PP_FFW:PP_FFW + 66] = ffw.reshape(3, 22, 128).transpose(2, 1, 0).reshape(128, 66)
    pp[:, PP_FFB:PP_FFB + 22] = _fm(inp["ffn_dw_b"][l], 22)
    pp[:, PP_GN] = np.asarray(inp["gdn_norm"][l], np.float32)
    pp[:, PP_NFIN:PP_NFIN + 8] = _fm(inp["norm_final"], 8)
    pb = np.zeros((128, 8), np.float32)
    pb[:, 0:4] = np.asarray(inp["gdn_dt_bias"][l], np.float32)[None, :]
    pb[:, 4:8] = np.asarray(inp["gdn_a_log"][l], np.float32)[None, :]
    return {f"wa{l}": wa, f"wd{l}": wd, f"wab{l}": wab, f"pp{l}": pp, f"pb{l}": pb}


def _x_to_fm(xb):
    T = xb.shape[0]
    return np.ascontiguousarray(xb.reshape(T // TB, TB, KC, 128).transpose(0, 3, 2, 1).reshape(T // TB, 128, KC * TB))


def _fm_to_x(o, T):
    return o.reshape(T // TB, 128, KC, TB).transpose(0, 3, 2, 1).reshape(T, D)


def run(inputs, n_batch=None, T=None, NL=None, dbg_names=(), trace=False):
    x = np.asarray(inputs["x"], np.float32)
    mem = np.asarray(inputs["mem"], np.float32)
    Bn, Sn, _ = x.shape
    T = Sn if T is None else T
    NL = inputs["w_in"].shape[0] if NL is None else NL
    nc = build(T, NL, dbg_names)
    shared = {"cst": _consts()}
    for l in range(NL):
        shared.update(_prep_layer(inputs, l))
    n_cores = 2 * Bn if SPREAD else Bn
    zero_map = None
    in_maps = []
    for c in range(n_cores):
        b = c // 2 if SPREAD else c
        if SPREAD and c % 2 == 1:
            if zero_map is None:
                zero_map = {k: np.zeros_like(v) for k, v in in_maps[0].items()}
            in_maps.append(zero_map)
            continue
        m = dict(shared)
        m["x"] = _x_to_fm(x[b, :T])
        m["mem"] = np.ascontiguousarray(mem[b].reshape(2, 128, D))
        in_maps.append(m)
    res = run_bass_kernel_spmd(nc, in_maps, core_ids=list(range(n_cores)), trace=trace)
    step = 2 if SPREAD else 1
    out = np.stack([_fm_to_x(np.asarray(res.results[step * b]["out"]), T) for b in range(Bn)], axis=0)
    return out, res


def kernel(**inputs):
    out, _ = run(inputs)
    return out.astype(np.float32)
```
